# Optimizing a Trainium2 kernel written in Bass

```python
import jax, jax.numpy as jnp
from jax import lax
import numpy as np

D_MODEL = 1024
BATCH = 2
SEQ = 8192
DEPTH = 4

HD = 64
QBLK = 128
A_GROUPS = 3
A_HEADS = 8
A_PATTERNS = ((128, 1), (512, 4), (2048, 16))
A_QKV_W = A_GROUPS * A_HEADS * HD
BRANCH_W = A_HEADS * HD
B_HEADS = 8
B_KV = 2
NSA_CMP_BLK = 32
NSA_CMP_STRIDE = 16
NSA_CMP_HIDDEN = 256
NSA_SEL_BLK = 64
NSA_N_SEL = 16
NSA_WINDOW = 512
C_HEADS = 8
C_KV = 2
C_WINDOW = 128
N_BRANCH = 3
DEEPNORM_ALPHA = (2 * DEPTH) ** 0.25
DEEPNORM_BETA = (8 * DEPTH) ** -0.25
LN_EPS = 1e-5
NEG = -1e30
FORCE_SCORE = 1e4
ATTN_SCALE = HD ** -0.5
IN_WIDTHS = (A_QKV_W, A_QKV_W, A_QKV_W, BRANCH_W,
             B_HEADS * HD, B_KV * HD, B_KV * HD, B_KV * HD, B_KV * HD, B_KV * HD, B_KV * HD,
             BRANCH_W, B_HEADS * 3,
             C_HEADS * HD, C_KV * HD, C_KV * HD, BRANCH_W,
             N_BRANCH * D_MODEL)
IN_COLS = sum(IN_WIDTHS)

kernel_name = "hybrid_dilated_nsa_swa_gated_deepnorm"


def alibi_slopes(n):
    return 2.0 ** (-8.0 * jnp.arange(1, n + 1, dtype=jnp.float32) / n)


def layer_norm(x, g, b):
    xf = x.astype(jnp.float32)
    mu = jnp.mean(xf, axis=-1, keepdims=True)
    var = jnp.mean(jnp.square(xf - mu), axis=-1, keepdims=True)
    return ((xf - mu) * lax.rsqrt(var + LN_EPS) * g.astype(jnp.float32) + b.astype(jnp.float32)).astype(x.dtype)


def banded_attention(q, k, v, slopes, window, dist_scale, sinks=None):
    B, L, H, Dh = q.shape
    Hkv = k.shape[2]
    G = H // Hkv
    nblk = -(-L // QBLK)
    Lp = nblk * QBLK
    nb = -(-window // QBLK)
    pad = Lp - L
    qp = jnp.pad(q, ((0, 0), (0, pad), (0, 0), (0, 0))).reshape(B, nblk, QBLK, Hkv, G, Dh)
    kp = jnp.pad(k, ((0, 0), (nb * QBLK, pad), (0, 0), (0, 0))).reshape(B, nblk + nb, QBLK, Hkv, Dh)
    vp = jnp.pad(v, ((0, 0), (nb * QBLK, pad), (0, 0), (0, 0))).reshape(B, nblk + nb, QBLK, Hkv, Dh)
    kc = jnp.concatenate([kp[:, i:i + nblk] for i in range(nb + 1)], axis=2)
    vc = jnp.concatenate([vp[:, i:i + nblk] for i in range(nb + 1)], axis=2)
    C = (nb + 1) * QBLK
    qi = jnp.arange(QBLK)[:, None]
    ci = jnp.arange(C)[None, :]
    dist = qi - ci + nb * QBLK
    kpos = jnp.arange(nblk)[:, None, None] * QBLK + ci[None] - nb * QBLK
    valid = (dist >= 0)[None] & (dist <= window)[None] & (kpos >= 0)
    s = jnp.einsum('bnikgd,bnckd->bnkgic', qp.astype(jnp.float32), kc.astype(jnp.float32)) * ATTN_SCALE
    s = s - (slopes.reshape(Hkv, G)[:, :, None, None] * dist_scale) * dist.astype(jnp.float32)
    s = jnp.where(valid[None, :, None, None], s, NEG)
    lse = jax.nn.logsumexp(s, axis=-1)
    if sinks is not None:
        lse = jnp.logaddexp(lse, sinks.astype(jnp.float32).reshape(Hkv, G)[:, :, None])
    p = jnp.exp(s - lse[..., None])
    o = jnp.einsum('bnkgic,bnckd->bnikgd', p, vc.astype(jnp.float32))
    o = o.reshape(B, Lp, H, Dh)[:, :L].astype(q.dtype)
    lse = lse.transpose(0, 1, 4, 2, 3).reshape(B, Lp, H)[:, :L]
    return o, lse


def dilated_attention(q, k, v, slopes, window, dilation):
    B, S, H, Dh = q.shape
    r = dilation

    def fold(t):
        return t.reshape(B, S // r, r, t.shape[2], Dh).transpose(0, 2, 1, 3, 4).reshape(B * r, S // r, t.shape[2], Dh)

    o, lse = banded_attention(fold(q), fold(k), fold(v), slopes, window // r, r)
    o = o.reshape(B, r, S // r, H, Dh).transpose(0, 2, 1, 3, 4).reshape(B, S, H, Dh)
    lse = lse.reshape(B, r, S // r, H).transpose(0, 2, 1, 3).reshape(B, S, H)
    return o, lse


def nsa_compress(x, w1, w2, pos):
    B, S, Hkv, Dh = x.shape
    ch = x.reshape(B, S // NSA_CMP_STRIDE, NSA_CMP_STRIDE, Hkv, Dh)
    blocks = jnp.concatenate([ch[:, :-1], ch[:, 1:]], axis=2)
    blocks = blocks + pos[None, None, :, None, :]
    ncmp = blocks.shape[1]
    flat = blocks.transpose(0, 1, 3, 2, 4).reshape(B, ncmp, Hkv, NSA_CMP_BLK * Dh)
    return jax.nn.gelu(flat @ w1) @ w2


def nsa_compressed_and_selected(q, kc, vc, ks, vs, slopes):
    B, S, H, Dh = q.shape
    Hkv = ks.shape[2]
    G = H // Hkv
    ncmp = kc.shape[1]
    nsel = S // NSA_SEL_BLK
    n_pick = min(NSA_N_SEL, nsel)
    nblk = S // QBLK
    cmp_start = jnp.arange(ncmp) * NSA_CMP_STRIDE
    cmp_end = cmp_start + NSA_CMP_BLK - 1
    sel_start = jnp.arange(nsel) * NSA_SEL_BLK
    overlap = ((cmp_start[:, None] < sel_start[None, :] + NSA_SEL_BLK)
               & (cmp_start[:, None] + NSA_CMP_BLK > sel_start[None, :])).astype(jnp.float32)
    slopes_kg = slopes.reshape(Hkv, G)
    kcf = kc.astype(jnp.float32)
    vcf = vc.astype(jnp.float32)
    kT = ks.transpose(0, 2, 1, 3)
    vT = vs.transpose(0, 2, 1, 3)
    bidx = jnp.arange(B)[:, None, None, None]
    hidx = jnp.arange(Hkv)[None, :, None, None]
    jsel = jnp.arange(nsel)[None, :]
    qb = q.reshape(B, nblk, QBLK, Hkv, G, Dh).transpose(1, 0, 2, 3, 4, 5)

    def block(args):
        n, qn = args
        t = n * QBLK + jnp.arange(QBLK)
        qf = qn.astype(jnp.float32)
        s = jnp.einsum('bikgd,bjkd->bkgij', qf, kcf) * ATTN_SCALE
        dist = (t[:, None] - cmp_end[None, :]).astype(jnp.float32)
        valid = dist >= 0
        s = jnp.where(valid, s - slopes_kg[:, :, None, None] * dist, NEG)
        m = jnp.max(s, axis=-1, keepdims=True)
        e = jnp.where(valid, jnp.exp(s - m), 0.0)
        den = jnp.sum(e, axis=-1, keepdims=True)
        p = e / jnp.where(den > 0, den, 1.0)
        o_cmp = jnp.einsum('bkgij,bjkd->bikgd', p, vcf)
        imp = jnp.einsum('bkgij,js->bkis', p, overlap)
        cur = (t // NSA_SEL_BLK)[:, None]
        forced = (jsel == 0) | (jsel == cur) | (jsel == cur - 1)
        allowed = jsel <= cur
        score = jnp.where(forced, FORCE_SCORE, jnp.where(allowed, imp, -1.0))
        _, idx = lax.top_k(score, n_pick)
        pos = (idx[..., None] * NSA_SEL_BLK + jnp.arange(NSA_SEL_BLK)).reshape(B, Hkv, QBLK, n_pick * NSA_SEL_BLK)
        ksel = kT[bidx, hidx, pos].astype(jnp.float32)
        vsel = vT[bidx, hidx, pos].astype(jnp.float32)
        s2 = jnp.einsum('bikgd,bkitd->bkgit', qf, ksel) * ATTN_SCALE
        d2 = (t[None, None, :, None] - pos)[:, :, None]
        s2 = s2 - slopes_kg[None, :, :, None, None] * d2.astype(jnp.float32)
        s2 = jnp.where(d2 >= 0, s2, NEG)
        p2 = jax.nn.softmax(s2, axis=-1)
        o_sel = jnp.einsum('bkgit,bkitd->bikgd', p2, vsel)
        return o_cmp, o_sel

    o_cmp, o_sel = lax.map(block, (jnp.arange(nblk), qb))
    o_cmp = o_cmp.transpose(1, 0, 2, 3, 4, 5).reshape(B, S, H, Dh).astype(q.dtype)
    o_sel = o_sel.transpose(1, 0, 2, 3, 4, 5).reshape(B, S, H, Dh).astype(q.dtype)
    return o_cmp, o_sel


def setup_inputs(seed: int = 0) -> dict:
    key = jax.random.key(seed)
    ks = jax.random.split(key, 11)
    x = jax.random.normal(ks[0], (BATCH, SEQ, D_MODEL), jnp.float32)
    w_in = jax.random.normal(ks[1], (DEPTH, D_MODEL, IN_COLS), jnp.float32) * D_MODEL ** -0.5
    b_in = 0.02 * jax.random.normal(ks[2], (DEPTH, IN_COLS), jnp.float32)
    w_cmp1 = jax.random.normal(ks[3], (DEPTH, 2, NSA_CMP_BLK * HD, NSA_CMP_HIDDEN), jnp.float32) * (NSA_CMP_BLK * HD) ** -0.5
    w_cmp2 = jax.random.normal(ks[4], (DEPTH, 2, NSA_CMP_HIDDEN, HD), jnp.float32) * NSA_CMP_HIDDEN ** -0.5
    cmp_pos = 0.02 * jax.random.normal(ks[5], (DEPTH, 2, NSA_CMP_BLK, HD), jnp.float32)
    sinks = 0.5 * jax.random.normal(ks[6], (DEPTH, C_HEADS), jnp.float32)
    w_branch = jax.random.normal(ks[7], (DEPTH, N_BRANCH, BRANCH_W, D_MODEL), jnp.float32) * (BRANCH_W ** -0.5 * DEEPNORM_BETA)
    w_out = jax.random.normal(ks[8], (DEPTH, D_MODEL, D_MODEL), jnp.float32) * (D_MODEL ** -0.5 * DEEPNORM_BETA)
    ln_g = 1.0 + 0.02 * jax.random.normal(ks[9], (DEPTH, D_MODEL), jnp.float32)
    ln_b = 0.02 * jax.random.normal(ks[10], (DEPTH, D_MODEL), jnp.float32)
    return {"x": x, "w_in": w_in, "b_in": b_in, "w_cmp1": w_cmp1, "w_cmp2": w_cmp2,
            "cmp_pos": cmp_pos, "sinks": sinks, "w_branch": w_branch, "w_out": w_out,
            "ln_g": ln_g, "ln_b": ln_b}


def reference(x, w_in, b_in, w_cmp1, w_cmp2, cmp_pos, sinks, w_branch, w_out, ln_g, ln_b):
    B, S, _ = x.shape
    split_at = np.cumsum(IN_WIDTHS)[:-1].tolist()
    a_slopes = alibi_slopes(A_GROUPS * A_HEADS).reshape(A_GROUPS, A_HEADS)
    b_slopes = alibi_slopes(B_HEADS)
    c_slopes = alibi_slopes(C_HEADS)
    for l in range(DEPTH):
        h = x @ w_in[l] + b_in[l]
        (aq, ak, av, ag, bq, bck, bcv, bsk, bsv, bwk, bwv, bg, bgate,
         cq, ck, cv, cg, mg) = jnp.split(h, split_at, axis=-1)

        aq = aq.reshape(B, S, A_GROUPS, A_HEADS, HD)
        ak = ak.reshape(B, S, A_GROUPS, A_HEADS, HD)
        av = av.reshape(B, S, A_GROUPS, A_HEADS, HD)
        outs, lses = [], []
        for gi, (win, dil) in enumerate(A_PATTERNS):
            o, lse = dilated_attention(aq[:, :, gi], ak[:, :, gi], av[:, :, gi], a_slopes[gi], win, dil)
            outs.append(o)
            lses.append(lse)
        wts = jax.nn.softmax(jnp.stack(lses), axis=0)
        ya = jnp.sum(wts[..., None] * jnp.stack(outs).astype(jnp.float32), axis=0)
        ya = ya.reshape(B, S, BRANCH_W).astype(x.dtype) * jax.nn.silu(ag)

        bq4 = bq.reshape(B, S, B_HEADS, HD)
        kcmp = nsa_compress(bck.reshape(B, S, B_KV, HD), w_cmp1[l, 0], w_cmp2[l, 0], cmp_pos[l, 0])
        vcmp = nsa_compress(bcv.reshape(B, S, B_KV, HD), w_cmp1[l, 1], w_cmp2[l, 1], cmp_pos[l, 1])
        o_cmp, o_sel = nsa_compressed_and_selected(bq4, kcmp, vcmp, bsk.reshape(B, S, B_KV, HD),
                                                   bsv.reshape(B, S, B_KV, HD), b_slopes)
        o_win, _ = banded_attention(bq4, bwk.reshape(B, S, B_KV, HD), bwv.reshape(B, S, B_KV, HD),
                                    b_slopes, NSA_WINDOW - 1, 1)
        gts = jax.nn.sigmoid(bgate.reshape(B, S, B_HEADS, 3))
        yb = gts[..., 0:1] * o_cmp + gts[..., 1:2] * o_sel + gts[..., 2:3] * o_win
        yb = yb.reshape(B, S, BRANCH_W).astype(x.dtype) * jax.nn.silu(bg)

        o_c, _ = banded_attention(cq.reshape(B, S, C_HEADS, HD), ck.reshape(B, S, C_KV, HD),
                                  cv.reshape(B, S, C_KV, HD), c_slopes, C_WINDOW - 1, 1, sinks=sinks[l])
        yc = o_c.reshape(B, S, BRANCH_W).astype(x.dtype) * jax.nn.silu(cg)

        mg = jax.nn.sigmoid(mg.reshape(B, S, N_BRANCH, D_MODEL))
        merged = (mg[:, :, 0] * (ya @ w_branch[l, 0])
                  + mg[:, :, 1] * (yb @ w_branch[l, 1])
                  + mg[:, :, 2] * (yc @ w_branch[l, 2]))
        y = (merged @ w_out[l]).astype(x.dtype)

        x = layer_norm(DEEPNORM_ALPHA * x + y, ln_g[l], ln_b[l])
    return x
```

```python
import numpy as np
import ml_dtypes
from contextlib import ExitStack
import concourse.bass as bass
import concourse.mybir as mybir
from concourse.bass_utils import run_bass_kernel_spmd

F32 = mybir.dt.float32
BF16 = mybir.dt.bfloat16
AF = mybir.ActivationFunctionType
ALU = mybir.AluOpType
NPBF = ml_dtypes.bfloat16

D = 1024
SEQ = 8192
NBATCH = 2
DEPTH = 4
T = 2048
NQT = 16
NCORE = 8
ALPHA = (2 * DEPTH) ** 0.25
LN_EPS = 1e-5
O_AQ, O_AK, O_AV, O_AG = 0, 1536, 3072, 4608
O_BQ, O_BCK, O_BCV, O_BSK, O_BSV, O_BWK, O_BWV, O_BG, O_BGATE = 5120, 5632, 5760, 5888, 6016, 6144, 6272, 6400, 6912
O_CQ, O_CK, O_CV, O_CG, O_MG = 6936, 7448, 7576, 7704, 8216
A_DIL = (1, 4, 16)


class Buf:
    __slots__ = ("name", "w", "r", "sem", "ndma")

    def __init__(self, name):
        self.name = name
        self.w = None
        self.r = {}
        self.sem = None
        self.ndma = 0


class Ctx:
    def __init__(self, nc, es, n_dma_sems=100):
        self.nc = nc
        self.es = es
        self.sems = []
        self.engs = {}
        for name, eng in (("pe", nc.tensor), ("act", nc.scalar), ("dve", nc.vector),
                          ("pool", nc.gpsimd), ("sp", nc.sync)):
            k = self._newsem(name + "_sem")
            self.engs[name] = dict(eng=eng, sem=k, cnt=0, seen={})
        self.free_dma = []
        self.dma_latest = {}
        self.nwaits = 0
        self.nops = 0

    def _newsem(self, name):
        h = self.es.enter_context(self.nc.semaphore(name))
        self.sems.append(h)
        return len(self.sems) - 1

    def _deps(self, reads, writes):
        deps = {}
        for b in reads:
            if b.w is not None:
                deps[b.w[0]] = max(deps.get(b.w[0], 0), b.w[1])
        for b in writes:
            if b.w is not None:
                deps[b.w[0]] = max(deps.get(b.w[0], 0), b.w[1])
            for k, v in b.r.items():
                deps[k] = max(deps.get(k, 0), v)
        return deps

    def _wait(self, e, deps):
        for k, v in deps.items():
            if e["seen"].get(k, 0) >= v:
                continue
            e["eng"].wait_ge(self.sems[k], v)
            e["seen"][k] = v
            self.nwaits += 1

    def _record(self, rec, reads, writes):
        for b in reads:
            b.r[rec[0]] = max(b.r.get(rec[0], 0), rec[1])
        for b in writes:
            b.w = rec
            b.r = {}

    def op(self, ename, fn, reads=(), writes=()):
        e = self.engs[ename]
        deps = self._deps(reads, writes)
        if ename == "pe":
            deps.pop(e["sem"], None)
        self._wait(e, deps)
        ins = fn(e["eng"])
        ins.then_inc(self.sems[e["sem"]], 1)
        e["cnt"] += 1
        self.nops += 1
        self._record((e["sem"], e["cnt"]), reads, writes)
        return ins

    def dma(self, out, in_, reads=(), writes=(), queue="sp", owner=None, **kw):
        e = self.engs[queue]
        if owner is None:
            owner = (list(writes) + list(reads))[0]
        if owner.sem is None:
            if self.free_dma:
                owner.sem, owner.ndma = self.free_dma.pop()
            else:
                owner.sem = self._newsem("dma%d" % len(self.sems))
        deps = self._deps(reads, writes)
        self._wait(e, deps)
        ins = e["eng"].dma_start(out=out, in_=in_, **kw)
        ins.then_inc(self.sems[owner.sem], 16)
        owner.ndma += 1
        self.nops += 1
        self._record((owner.sem, 16 * owner.ndma), reads, writes)
        self.dma_latest[owner.sem] = 16 * owner.ndma
        return ins

    def barrier(self):
        deps = dict(self.dma_latest)
        for e in self.engs.values():
            if e["cnt"]:
                deps[e["sem"]] = e["cnt"]
        for e in self.engs.values():
            self._wait(e, deps)

    def retire(self, bufs):
        for b in bufs:
            if b.sem is not None:
                self.free_dma.append((b.sem, b.ndma))
                b.sem = None

    def wait_all(self, ename, bufs):
        e = self.engs[ename]
        self._wait(e, self._deps((), bufs))


class Rot:
    def __init__(self, items):
        self.items = items
        self.i = 0

    def next(self):
        it = self.items[self.i % len(self.items)]
        self.i += 1
        return it


class Env:
    def __init__(self, nc, es, cx):
        self.nc, self.es, self.cx = nc, es, cx
        self.n = 0

    def sb(self, shape, dt, name=None):
        self.n += 1
        name = "%s_s%d" % (name or "t", self.n)
        t = self.es.enter_context(self.nc.sbuf_tensor(name, shape, dt))
        return t, Buf(name)

    def ps(self, name):
        t = self.es.enter_context(self.nc.psum_tensor(name, [128, 512], F32))
        return t, Buf(name)

    def din(self, name, shape, dt):
        return self.nc.dram_tensor(name, list(shape), dt, kind="ExternalInput").ap()

    def dout(self, name, shape, dt):
        return self.nc.dram_tensor(name, list(shape), dt, kind="ExternalOutput").ap(), Buf(name)


def load_xT(env, xT, ntok, xb, xbB):
    cx = env.cx
    stg = Rot([env.sb([128, ntok], F32) for _ in range(2)])
    for kc in range(8):
        s, sB = stg.next()
        cx.dma(s[:], xT[kc], writes=[sB])
        eng = "dve" if kc % 2 == 0 else "pool"
        cx.op(eng, lambda e: e.tensor_copy(out=xb[:, kc, :], in_=s[:]), reads=[sB], writes=[xbB])


class FMProj:
    def __init__(self, env, xb, xbB, psbanks, mmax=128):
        self.env, self.xb, self.xbB = env, xb, xbB
        self.wf = Rot([env.sb([128, 8, mmax], F32) for _ in range(3)])
        self.wb = Rot([env.sb([128, 8, mmax], BF16) for _ in range(3)])
        self.ps = Rot(psbanks)
        self.ncast = 0

    def run(self, chunks, tiles, consume):
        cx = self.env.cx
        loaded = []

        def load(j):
            ap, m, tag = chunks[j]
            f, fB = self.wf.next()
            cx.dma(f[:, :, 0:m], ap, writes=[fB])
            loaded.append((f, fB))

        def cast(j):
            ap, m, tag = chunks[j]
            f, fB = loaded[j]
            b, bB = self.wb.next()
            eng = "pool" if self.ncast % 3 != 2 else "dve"
            self.ncast += 1
            cx.op(eng, lambda e: e.tensor_copy(out=b[:, :, 0:m], in_=f[:, :, 0:m]), reads=[fB], writes=[bB])
            return b, bB

        n = len(chunks)
        casted = {}
        for j in range(min(2, n)):
            load(j)
        casted[0] = cast(0)
        for j in range(n):
            if j + 2 < n:
                load(j + 2)
            if j + 1 < n:
                casted[j + 1] = cast(j + 1)
            ap, m, tag = chunks[j]
            b, bB = casted.pop(j)
            for ti, (t0, nt) in enumerate(tiles):
                p, pB = self.ps.next()
                for kc in range(8):
                    cx.op("pe", lambda e: e.matmul(p[0:m, 0:nt], lhsT=b[:, kc, 0:m], rhs=self.xb[:, kc, t0:t0 + nt],
                                                   start=(kc == 0), stop=(kc == 7)),
                          reads=[bB, self.xbB], writes=[pB])
                consume(tag, ti, t0, nt, p, pB)


class TMProj:
    def __init__(self, env, xb, xbB, psbanks):
        self.env, self.xb, self.xbB = env, xb, xbB
        self.wf, self.wfB = env.sb([128, 8, 512], F32)
        self.wb, self.wbB = env.sb([128, 8, 512], BF16)
        self.ps = Rot(psbanks)

    def run(self, w_ap, ncols, ntiles, consume):
        cx = self.env.cx
        cx.dma(self.wf[:, :, 0:ncols], w_ap, writes=[self.wfB])
        for kc in range(8):
            eng = "pool" if kc % 2 == 0 else "dve"
            cx.op(eng, lambda e: e.tensor_copy(out=self.wb[:, kc, 0:ncols], in_=self.wf[:, kc, 0:ncols]),
                  reads=[self.wfB], writes=[self.wbB])
        for i in range(ntiles):
            p, pB = self.ps.next()
            for kc in range(8):
                cx.op("pe", lambda e: e.matmul(p[:, 0:ncols], lhsT=self.xb[:, kc, i * 128:(i + 1) * 128],
                                               rhs=self.wb[:, kc, 0:ncols], start=(kc == 0), stop=(kc == 7)),
                      reads=[self.wbB, self.xbB], writes=[pB])
            consume(i, p, pB)


P1_NFM = 17
P1_NTM = 4
P1_TMW = (512, 512, 512, 384)


def build_p1():
    nc = bass.Bass("TRN2", target_bir_lowering=False)
    es = ExitStack()
    with es:
        cx = Ctx(nc, es)
        env = Env(nc, es, cx)
        xT = env.din("xT", [8, 128, T + 16], F32)
        wfm = env.din("wfm", [P1_NFM, 128, 8, 128], F32)
        bfm = env.din("bfm", [128, P1_NFM], F32)
        wtm = env.din("wtm", [P1_NTM, 128, 8, 512], F32)
        btm = env.din("btm", [P1_NTM, 128, 512], F32)
        w1 = env.din("w1", [2, 64, 32, 256], F32)
        w2 = env.din("w2", [2, 128, 2, 64], F32)
        pos = env.din("pos", [2, 128, 32], F32)
        KT, KTB = env.dout("KT", [15, 128, T], BF16)
        V, VB = env.dout("V", [T, 1920], BF16)
        kcT, kcTB = env.dout("kcT", [128, 128], BF16)
        vc, vcB = env.dout("vc", [128, 128], BF16)

        outs = []
        xb, xbB = env.sb([128, 8, T + 16], BF16, "xb")
        load_xT(env, xT, T + 16, xb, xbB)
        bfm_s, bfmB = env.sb([128, P1_NFM], F32, "bfm_s")
        cx.dma(bfm_s[:], bfm, writes=[bfmB])
        banks = [env.ps("pa%d" % i) for i in range(4)]
        pcs = [env.ps("pc0"), env.ps("pc1")]
        pd, pdB = env.ps("pd")

        fm = FMProj(env, xb, xbB, banks)
        kts = Rot([env.sb([128, T], BF16) for _ in range(2)])
        bc, bcB = env.sb([128, 2, T + 16], BF16, "bc")
        cur = {}

        def consume_fm(tag, ti, t0, nt, p, pB):
            if tag < 15:
                if ti == 0:
                    cur["kt"] = kts.next()
                k, kB = cur["kt"]
                cx.op("act", lambda e: e.activation(out=k[:, t0:t0 + nt], in_=p[:, 0:nt], func=AF.Identity,
                                                    bias=bfm_s[:, tag:tag + 1], scale=1.0),
                      reads=[pB, bfmB], writes=[kB])
                if ti == 3:
                    ob = Buf("o"); outs.append(ob)
                    cx.dma(KT[tag], k[:], reads=[kB], writes=[ob], queue="pool", owner=kB)
            else:
                cx.op("act", lambda e: e.activation(out=bc[:, tag - 15, t0:t0 + nt], in_=p[:, 0:nt], func=AF.Identity,
                                                    bias=bfm_s[:, tag:tag + 1], scale=1.0),
                      reads=[pB, bfmB], writes=[bcB])

        tiles4 = [(i * 512, 512) for i in range(4)]
        fm.run([(wfm[c], 128, c) for c in range(15)], tiles4, consume_fm)
        fm.run([(wfm[c], 128, c) for c in (15, 16)], tiles4 + [(T, 16)], consume_fm)

        STAGE = 3
        tm = TMProj(env, xb, xbB, banks)
        bt_s, btB = env.sb([128, 512], F32, "bt_s")
        vs = Rot([env.sb([128, 512], BF16) for _ in range(3)])
        coff = 0
        for gi in range(P1_NTM if STAGE >= 2 else 0):
            ncols = P1_TMW[gi]
            cx.dma(bt_s[:, 0:ncols], btm[gi, :, 0:ncols], writes=[btB])

            def consume_tm(i, p, pB, ncols=ncols, coff=coff):
                v, vB = vs.next()
                cx.op("dve", lambda e: e.tensor_tensor(out=v[:, 0:ncols], in0=p[:, 0:ncols], in1=bt_s[:, 0:ncols], op=ALU.add),
                      reads=[pB, btB], writes=[vB])
                ob = Buf("o"); outs.append(ob)
                cx.dma(V[i * 128:(i + 1) * 128, coff:coff + ncols], v[:, 0:ncols], reads=[vB], writes=[ob],
                       queue="pool", owner=vB)

            tm.run(wtm[gi, :, :, 0:ncols], ncols, NQT, consume_tm)
            coff += ncols

        w1f, w1fB = env.sb([128, 32, 256], F32, "w1f")
        w1b, w1bB = env.sb([128, 32, 256], BF16, "w1b")
        w2f, w2fB = env.sb([128, 2, 64], F32, "w2f")
        w2b, w2bB = env.sb([128, 2, 64], BF16, "w2b")
        pos_s, posB = env.sb([128, 32], F32, "pos_s")
        tmp, tmpB = env.sb([128, 32, 128], BF16, "cmp_tmp")
        xs, xsB = env.sb([128, 512], F32, "cmp_xs")
        u1, u1B = env.sb([128, 512], F32, "cmp_u1")
        u2, u2B = env.sb([128, 512], F32, "cmp_u2")
        gl, glB = env.sb([128, 4, 128], BF16, "cmp_gl")
        kc_s, kcsB = env.sb([64, 2, 128], BF16, "kc_s")
        vc_s, vcsB = env.sb([128, 128], BF16, "vc_s")
        for kind in range(2 if STAGE >= 3 else 0):
            for half in range(2):
                cx.dma(w1f[64 * half:64 * half + 64], w1[kind], writes=[w1fB])
            cx.dma(w2f[:], w2[kind], writes=[w2fB])
            cx.dma(pos_s[:], pos[kind], writes=[posB])
            for q4 in range(4):
                eng = "pool" if q4 % 2 == 0 else "dve"
                cx.op(eng, lambda e: e.tensor_copy(out=w1b[:, q4 * 8:(q4 + 1) * 8, :], in_=w1f[:, q4 * 8:(q4 + 1) * 8, :]),
                      reads=[w1fB], writes=[w1bB])
            cx.op("dve", lambda e: e.tensor_copy(out=w2b[:], in_=w2f[:]), reads=[w2fB], writes=[w2bB])
            for p_ in range(32):
                cx.op("dve", lambda e: e.tensor_scalar(out=tmp[:, p_, :], in0=bc[:, kind, p_:p_ + 16 * 127 + 1:16],
                                                       scalar1=pos_s[:, p_:p_ + 1], scalar2=None, op0=ALU.add),
                      reads=[bcB, posB], writes=[tmpB])
            SUB = 9
            if SUB < 1:
                continue
            for g in range(2):
                for m in range(2):
                    sl = m * 128
                    pc, pcB = pcs[g]
                    for p_ in range(32):
                        cx.op("pe", lambda e: e.matmul(pc[:, sl:sl + 128], lhsT=w1b[64 * g:64 * g + 64, p_, m * 128:(m + 1) * 128],
                                                       rhs=tmp[64 * g:64 * g + 64, p_, :], start=(p_ == 0), stop=(p_ == 31)),
                              reads=[w1bB, tmpB], writes=[pcB])
            if SUB < 2:
                continue
            for g in range(2):
                pc, pcB = pcs[g]
                cx.op("act", lambda e: e.activation(out=xs[:, g * 256:(g + 1) * 256], in_=pc[:, 0:256], func=AF.Identity, scale=1.0),
                      reads=[pcB], writes=[xsB])
            cx.op("dve", lambda e: e.tensor_tensor(out=u1[:], in0=xs[:], in1=xs[:], op=ALU.mult), reads=[xsB], writes=[u1B])
            cx.op("dve", lambda e: e.tensor_scalar(out=u1[:], in0=u1[:], scalar1=0.044715, scalar2=1.0, op0=ALU.mult, op1=ALU.add),
                  reads=[u1B], writes=[u1B])
            cx.op("dve", lambda e: e.tensor_tensor(out=u2[:], in0=u1[:], in1=xs[:], op=ALU.mult), reads=[u1B, xsB], writes=[u2B])
            cx.op("act", lambda e: e.activation(out=u1[:], in_=u2[:], func=AF.Sigmoid, scale=1.5957691216057308),
                  reads=[u2B], writes=[u1B])
            cx.op("dve", lambda e: e.tensor_tensor(out=gl[:].rearrange("p a b -> p (a b)"), in0=xs[:], in1=u1[:], op=ALU.mult),
                  reads=[xsB, u1B], writes=[glB])
            if SUB < 3:
                continue
            for g in range(2):
                if kind == 0:
                    for m in range(2):
                        cx.op("pe", lambda e: e.matmul(pd[0:64, g * 128:(g + 1) * 128], lhsT=w2b[:, m, :], rhs=gl[:, g * 2 + m, :],
                                                       start=(m == 0), stop=(m == 1)), reads=[w2bB, glB], writes=[pdB])
                else:
                    for m in range(2):
                        cx.op("pe", lambda e: e.matmul(pd[:, 256 + g * 64:256 + (g + 1) * 64], lhsT=gl[:, g * 2 + m, :], rhs=w2b[:, m, :],
                                                       start=(m == 0), stop=(m == 1)), reads=[w2bB, glB], writes=[pdB])
            if kind == 0:
                cx.op("dve", lambda e: e.tensor_copy(out=kc_s[:].rearrange("p a b -> p (a b)"), in_=pd[0:64, 0:256]),
                      reads=[pdB], writes=[kcsB])
                for g in range(2):
                    ob = Buf("o"); outs.append(ob)
                    cx.dma(kcT[64 * g:64 * g + 64, :], kc_s[:, g, :], reads=[kcsB], writes=[ob], queue="pool", owner=kcsB)
            else:
                cx.op("dve", lambda e: e.tensor_copy(out=vc_s[:], in_=pd[:, 256:384]), reads=[pdB], writes=[vcsB])
                ob = Buf("o"); outs.append(ob)
                cx.dma(vc, vc_s[:], reads=[vcsB], writes=[ob], queue="pool", owner=vcsB)
        cx.wait_all("pool", outs)
        cx.wait_all("sp", outs)
    return nc


def _fm_tile(w, cols):
    return np.ascontiguousarray(w[:, cols].reshape(8, 128, len(cols)).transpose(1, 0, 2))


def p1_weights(w_in_l, b_in_l, w_cmp1_l, w_cmp2_l, cmp_pos_l):
    ar = np.arange
    fm_cols = [O_AK + 128 * c + ar(128) for c in range(12)] + [O_BSK + ar(128), O_BWK + ar(128), O_CK + ar(128),
                                                               O_BCK + ar(128), O_BCV + ar(128)]
    wfm = np.stack([_fm_tile(w_in_l, c) for c in fm_cols])
    bfm = np.ascontiguousarray(np.stack([b_in_l[c] for c in fm_cols], axis=1))
    tm_cols = [O_AV + 512 * g + ar(512) for g in range(3)] + [np.concatenate([O_BSV + ar(128), O_BWV + ar(128), O_CV + ar(128)])]
    wtm = np.zeros((P1_NTM, 128, 8, 512), np.float32)
    btm = np.zeros((P1_NTM, 128, 512), np.float32)
    for i, c in enumerate(tm_cols):
        wtm[i, :, :, :len(c)] = _fm_tile(w_in_l, c)
        btm[i, :, :len(c)] = b_in_l[c][None, :]
    w1 = np.ascontiguousarray(w_cmp1_l.reshape(2, 32, 64, 256).transpose(0, 2, 1, 3))
    w2 = np.ascontiguousarray(w_cmp2_l.reshape(2, 2, 128, 64).transpose(0, 2, 1, 3))
    posT = cmp_pos_l.transpose(0, 2, 1)
    pos = np.ascontiguousarray(np.concatenate([posT, posT], axis=1))
    return dict(wfm=wfm, bfm=bfm, wtm=wtm, btm=btm, w1=w1, w2=w2, pos=pos)


def xT_for_core(x_b, j, halo):
    n = T + halo
    seg = np.zeros((n, D), np.float32)
    hi = min(SEQ, j * T + n)
    seg[:hi - j * T] = x_b[j * T:hi]
    return np.ascontiguousarray(seg.T.reshape(8, 128, n))


_NC_CACHE = {}


def get_nc(name, builder):
    if name not in _NC_CACHE:
        _NC_CACHE[name] = builder()
    return _NC_CACHE[name]


def run_p1(x_full, wts):
    nc = get_nc("p1", build_p1)
    in_maps = []
    for c in range(NCORE):
        b, j = divmod(c, 4)
        m = dict(wts)
        m["xT"] = xT_for_core(x_full[b], j, 16)
        in_maps.append(m)
    res = run_bass_kernel_spmd(nc, in_maps, core_ids=list(range(NCORE)))
    return res.results


A_NT = (17, 20, 32)
P2_NFM_A = 16
SEL_T0 = 48


def a_qtiles(g):
    r = A_DIL[g]
    nu = 16 // r
    return r, nu, [(c, u) for c in range(r) for u in range(nu)]


class St:
    pass


def build_p2(stop_after=None):
    nc = bass.Bass("TRN2", target_bir_lowering=False)
    es = ExitStack()
    with es:
        st = St()
        st.nc, st.es = nc, es
        st.cx = cx = Ctx(nc, es)
        st.env = env = Env(nc, es, cx)
        din = env.din
        st.xT = din("xT", [8, 128, T], F32)
        identf_d = din("ident", [128, 128], F32)
        st.wA = din("wA", [P2_NFM_A, 128, 8, 128], F32)
        st.bA = din("bA", [128, P2_NFM_A], F32)
        st.KA = [din("KA%d" % g, [4, 128, A_NT[g] * 128], BF16) for g in range(3)]
        st.VA = [din("VA%d" % g, [A_NT[g], 128, 512], BF16) for g in range(3)]
        st.TBA = din("TBA", [4, 128, 3 * 2 * 4 * 128], F32)
        st.wQ = din("wQ", [16, 128, 8, 64], F32)
        st.bQ = din("bQ", [64, 16], F32)
        st.wT = din("wT", [3, 128, 8, 512], F32)
        st.bT = din("bT", [3, 128, 512], F32)
        st.QAUG = din("QAUG", [4, 2 * 16 * 4 * 128], BF16)
        st.Ksel = din("Ksel", [2, 68, 8192], BF16)
        st.Vsel = din("Vsel", [64, 128, 2, 65], BF16)
        st.Kwin = din("Kwin", [2, 68, 2560], BF16)
        st.Vwin = din("Vwin", [20, 128, 2, 65], BF16)
        st.Kcmp = din("Kcmp", [2, 68, 512], BF16)
        st.Vcmp = din("Vcmp", [4, 128, 2, 193], BF16)
        st.Kc = din("Kc", [2, 68, 2176], BF16)
        st.Vc = din("Vc", [17, 128, 2, 65], BF16)
        st.Mcmp = din("Mcmp", [128, 17, 128], BF16)
        st.CAUS = din("CAUS", [128, 2, 128], BF16)
        st.ALF = din("ALF", [16, 128, 2, 128], F32)
        st.EXP = din("EXP", [128, 8192], BF16)
        st.sinks = din("sinks", [128, 8], F32)
        st.wM = din("wM", [24, 128, 8, 128], F32)
        st.bM = din("bM", [128, 24], F32)
        st.wBr = din("wBr", [3, 128, 4, 1024], F32)
        st.wO = din("wO", [128, 8, 1024], F32)
        st.lng = din("lng", [128, 8], F32)
        st.lnb = din("lnb", [128, 8], F32)
        st.xnT, _ = env.dout("xnT", [8, 128, T], F32)
        st.yT, _ = env.dout("yT", [3, 4, 128, T], BF16)
        st.outs = []
        st.yB = [[Buf("yT%d_%d" % (b_, c_)) for c_ in range(4)] for b_ in range(3)]
        st.banks = [env.ps("ps%d" % i) for i in range(8)]
        ident_f, identfB = env.sb([128, 128], F32, "ident_f")
        st.ident, st.identB = env.sb([128, 128], BF16, "ident")
        st.ones_b, st.onesB = env.sb([128, 128], BF16, "ones_b")
        st.ones_f, st.onesfB = env.sb([128, 128], F32, "ones_f")
        cx.dma(ident_f[:], identf_d, writes=[identfB])
        cx.op("dve", lambda e: e.tensor_copy(out=st.ident[:], in_=ident_f[:]), reads=[identfB], writes=[st.identB])
        cx.op("pool", lambda e: e.memset(st.ones_b[:], 1.0), writes=[st.onesB])
        cx.op("pool", lambda e: e.memset(st.ones_f[:], 1.0), writes=[st.onesfB])
        phases = [("A", lambda: phase_A(st)), ("B", lambda: phase_gqa(st, True)), ("C", lambda: phase_gqa(st, False)),
                  ("F", lambda: phase_final(st))]
        for name, fn in phases:
            fn()
            cx.barrier()
            if stop_after == name:
                break
        allo = st.outs + [b for row in st.yB for b in row]
        cx.wait_all("pool", allo)
        cx.wait_all("sp", allo)
        st.es = None
    return nc


def phase_A(st):
    cx, env, banks = st.cx, st.env, st.banks
    with ExitStack() as ph:
        env.es = ph
        xb, xbB = env.sb([128, 8, T], BF16, "xbA")
        load_xT(env, st.xT, T, xb, xbB)
        bA_s, bAB = env.sb([128, P2_NFM_A], F32, "bA_s")
        cx.dma(bA_s[:], st.bA, writes=[bAB])
        fm = FMProj(env, xb, xbB, banks[0:2])
        qA = [env.sb([128, T], BF16, "qA%d" % g) for g in range(3)]
        agT, agB = env.sb([128, T], BF16, "agT")
        ka = [env.sb([128, A_NT[g] * 128], BF16, "ka%d" % g) for g in range(3)]
        va = [env.sb([128, A_NT[g], 128], BF16, "va%d" % g) for g in range(3)]
        tb, tbB = env.sb([128, 3, 2, 4, 128], F32, "tbA")
        Oacc, OaccB = env.sb([128, T], F32, "Oacc")
        Dacc, DaccB = env.sb([128, T], F32, "Dacc")
        yaT = Rot([env.sb([128, T], BF16) for _ in range(2)])
        Et = Rot([env.sb([128, 4, 128], BF16) for _ in range(3)])
        Pt = Rot([env.sb([128, 4, 128], BF16) for _ in range(3)])
        Sbk = [Rot(banks[2:4]), Rot(banks[4:6])]
        ODbk = Rot(banks[6:8])
        ones_b, onesB = st.ones_b, st.onesB
        tiles4 = [(i * 512, 512) for i in range(4)]
        for pr in range(4):
            def consume_a(tag, ti, t0, nt, p, pB):
                if tag < 12:
                    g = tag // 4
                    r = A_DIL[g]
                    q, qB_ = qA[g]
                    m0 = t0 // r
                    cx.op("act", lambda e: e.activation(
                        out=q[:].rearrange("p (c m) -> p m c", c=r)[:, m0:m0 + nt // r, :],
                        in_=p[:, 0:nt].rearrange("p (m c) -> p m c", c=r),
                        func=AF.Identity, bias=bA_s[:, tag:tag + 1], scale=1.0), reads=[pB, bAB], writes=[qB_])
                else:
                    cx.op("act", lambda e: e.activation(out=agT[:, t0:t0 + nt], in_=p[:, 0:nt], func=AF.Silu,
                                                        bias=bA_s[:, tag:tag + 1], scale=1.0), reads=[pB, bAB], writes=[agB])
            chunks = [(st.wA[g * 4 + pr], 128, g * 4 + pr) for g in range(3)] + [(st.wA[12 + pr], 128, 12 + pr)]
            fm.run(chunks, tiles4, consume_a)
            for g in range(3):
                cx.dma(ka[g][0][:], st.KA[g][pr], writes=[ka[g][1]])
                cx.dma(va[g][0][:], st.VA[g].rearrange("t k c -> k t c")[:, :, pr * 128:(pr + 1) * 128], writes=[va[g][1]])
            cx.dma(tb[:].rearrange("p a b c d -> p (a b c d)"), st.TBA[pr], writes=[tbB])
            cx.op("pool", lambda e: e.memset(Oacc[:], 0.0), writes=[OaccB])
            cx.op("pool", lambda e: e.memset(Dacc[:], 0.0), writes=[DaccB])
            for g in range(3):
                r, nu, qts = a_qtiles(g)
                q, qB_ = qA[g]
                k_, kB_ = ka[g]
                v_, vB_ = va[g]
                for hh in range(2):
                    ps_ = slice(64 * hh, 64 * hh + 64)
                    for un in range(8):
                        S, SB = Sbk[hh].next()
                        OD, ODB = ODbk.next()
                        E, EB = Et.next()
                        P, PB = Pt.next()
                        info = []
                        for a in range(2):
                            c, u = qts[2 * un + a]
                            qi = c * nu + u
                            kprev = c * (1 + nu) + u
                            info.append((c, u, kprev, kprev + 1))
                            for b_ in range(2):
                                kt = kprev + b_
                                cx.op("pe", lambda e: e.matmul(S[:, (2 * a + b_) * 128:(2 * a + b_ + 1) * 128],
                                                               lhsT=k_[ps_, kt * 128:(kt + 1) * 128],
                                                               rhs=q[ps_, qi * 128:(qi + 1) * 128], start=True, stop=True),
                                      reads=[kB_, qB_], writes=[SB])
                        cx.op("act", lambda e: e.activation(out=E[:].rearrange("p a b -> p (a b)"), in_=S[:], func=AF.Exp, scale=0.125),
                              reads=[SB], writes=[EB])
                        for a in range(2):
                            c, u, kp, kd = info[a]
                            ty = 0 if u == 0 else 2
                            cx.op("dve", lambda e: e.tensor_tensor(out=P[:, 2 * a:2 * a + 2, :], in0=E[:, 2 * a:2 * a + 2, :],
                                                                   in1=tb[:, g, hh, ty:ty + 2, :], op=ALU.mult),
                                  reads=[EB, tbB], writes=[PB])
                        for a in range(2):
                            c, u, kp, kd = info[a]
                            for b_, kt in enumerate((kp, kd)):
                                cx.op("pe", lambda e: e.matmul(OD[:, a * 128:(a + 1) * 128], lhsT=v_[:, kt, :], rhs=P[:, 2 * a + b_, :],
                                                               start=(b_ == 0), stop=(b_ == 1)), reads=[vB_, PB], writes=[ODB])
                            for b_ in range(2):
                                cx.op("pe", lambda e: e.matmul(OD[:, 256 + a * 128:256 + (a + 1) * 128], lhsT=ones_b[:], rhs=P[:, 2 * a + b_, :],
                                                               start=(b_ == 0), stop=(b_ == 1)), reads=[onesB, PB], writes=[ODB])
                        for a in range(2):
                            c, u, kp, kd = info[a]
                            t0 = c + 128 * u * r
                            tsl = slice(t0, t0 + 127 * r + 1, r)
                            cx.op("dve", lambda e: e.tensor_tensor(out=Oacc[ps_, tsl], in0=OD[ps_, a * 128:(a + 1) * 128],
                                                                   in1=Oacc[ps_, tsl], op=ALU.add), reads=[ODB, OaccB], writes=[OaccB])
                            cx.op("dve", lambda e: e.tensor_tensor(out=Dacc[ps_, tsl], in0=OD[ps_, 256 + a * 128:256 + (a + 1) * 128],
                                                                   in1=Dacc[ps_, tsl], op=ALU.add), reads=[ODB, DaccB], writes=[DaccB])
            ya, yaB = yaT.next()
            cx.op("dve", lambda e: e.reciprocal(out=Dacc[:], in_=Dacc[:]), reads=[DaccB], writes=[DaccB])
            cx.op("pool", lambda e: e.tensor_tensor(out=Oacc[:], in0=Oacc[:], in1=Dacc[:], op=ALU.mult),
                  reads=[OaccB, DaccB], writes=[OaccB])
            cx.op("pool", lambda e: e.tensor_tensor(out=ya[:], in0=Oacc[:], in1=agT[:], op=ALU.mult),
                  reads=[OaccB, agB], writes=[yaB])
            cx.dma(st.yT[0, pr], ya[:], reads=[yaB], writes=[st.yB[0][pr]], queue="pool", owner=yaB)
        cx.barrier()
        cx.retire([b for _, b in fm.wf.items] + [b for _, b in ka] + [b for _, b in va] + [tbB, bAB] + [b for _, b in yaT.items])
        env.es = st.es


def phase_gqa(st, isB):
    cx, env, banks = st.cx, st.env, st.banks
    hoff = 0 if isB else 8
    ybr = 1 if isB else 2
    with ExitStack() as ph:
        env.es = ph
        Qaug, QB = env.sb([68, 2 * 16 * 4 * 128], BF16, "Qaug")
        gS, gSB = env.sb([128, 16, 512], BF16, "gateS")
        gsig, gsigB = env.sb([128, 16, 24], F32, "gsig")
        with ExitStack() as ph2:
            env.es = ph2
            xb, xbB = env.sb([128, 8, T], BF16, "xbG")
            load_xT(env, st.xT, T, xb, xbB)
            bQ_s, bQB = env.sb([64, 16], F32, "bQ_s")
            cx.dma(bQ_s[:], st.bQ, writes=[bQB])
            fm = FMProj(env, xb, xbB, banks[2:4], mmax=64)
            Qv = Qaug[:].rearrange("p (g i h q) -> p g i h q", g=2, i=16, h=4)

            def consume_q(tag, ti, t0, nt, p, pB):
                g, hh = divmod(tag, 4)
                cx.op("act", lambda e: e.activation(out=Qv[0:64, g, 4 * ti:4 * ti + 4, hh, :],
                                                    in_=p[0:64, 0:512].rearrange("p (i q) -> p i q", i=4),
                                                    func=AF.Identity, bias=bQ_s[:, hoff + tag:hoff + tag + 1], scale=1.0),
                      reads=[pB, bQB], writes=[QB])
            fm.run([(st.wQ[hoff + h], 64, h) for h in range(8)], [(i * 512, 512) for i in range(4)], consume_q)
            tm = TMProj(env, xb, xbB, banks[4:6])
            bt_s, btB = env.sb([128, 512], F32, "bt_s")
            tmpf = Rot([env.sb([128, 512], F32) for _ in range(2)])
            for (gi, ncols, func, dst) in ([(0, 512, AF.Silu, gS), (1, 24, AF.Sigmoid, gsig)] if isB else [(2, 512, AF.Silu, gS)]):
                cx.dma(bt_s[:, 0:ncols], st.bT[gi, :, 0:ncols], writes=[btB])
                dB = gSB if dst is gS else gsigB

                def consume_t(i, p, pB, ncols=ncols, func=func, dst=dst, dB=dB):
                    t_, tB_ = tmpf.next()
                    cx.op("dve", lambda e: e.tensor_tensor(out=t_[:, 0:ncols], in0=p[:, 0:ncols], in1=bt_s[:, 0:ncols], op=ALU.add),
                          reads=[pB, btB], writes=[tB_])
                    cx.op("act", lambda e: e.activation(out=dst[:, i, 0:ncols], in_=t_[:, 0:ncols], func=func), reads=[tB_], writes=[dB])
                tm.run(st.wT[gi, :, :, 0:ncols], ncols, NQT, consume_t)
            cx.barrier()
            cx.retire([b for _, b in fm.wf.items] + [tm.wfB, btB, bQB])
            env.es = ph
        cx.dma(Qaug[64:68, :], st.QAUG, writes=[QB])
        caus, causB = env.sb([128, 2, 128], BF16, "caus")
        cx.dma(caus[:], st.CAUS, writes=[causB])
        Sb = Rot(banks[0:2])
        Et = Rot([env.sb([128, 4, 128], BF16) for _ in range(3)])
        Pt = Rot([env.sb([128, 4, 128], BF16) for _ in range(3)])
        ybacc, ybaccB = env.sb([128, 4, 64], F32, "ybacc")
        ytmp, ytmpB = env.sb([128, 4, 64], F32, "ytmp")
        ybf, ybfB = env.sb([128, 256], BF16, "ybf")
        ybT = Rot([env.sb([128, 2, T], BF16) for _ in range(2)])
        dn, dnB = env.sb([128, 4], F32, "dn")
        fac, facB = env.sb([128, 4], F32, "fac")
        TR, TRB = banks[3]
        TRb = TR[:].bitcast(BF16)
        retire = [causB, QB]
        if isB:
            mcmp, mcmpB = env.sb([128, 17, 128], BF16, "mcmp")
            cx.dma(mcmp[:], st.Mcmp, writes=[mcmpB])
            expn, expnB = env.sb([128, 8192], BF16, "expn")
            cx.dma(expn[:], st.EXP, writes=[expnB])
            ksel, kselB = env.sb([68, 8192], BF16, "ksel")
            vsel, vselB = env.sb([128, 64, 65], BF16, "vsel")
            kwin, kwinB = env.sb([68, 2560], BF16, "kwin")
            vwin, vwinB = env.sb([128, 20, 65], BF16, "vwin")
            kcmp, kcmpB = env.sb([68, 512], BF16, "kcmp")
            vcmp, vcmpB = env.sb([128, 4, 193], BF16, "vcmp")
            alf = Rot([env.sb([128, 2, 128], F32) for _ in range(2)])
            impacc, impB = env.sb([128, 128], F32, "impacc")
            score, scoreB = env.sb([128, 128], F32, "score")
            m8, m8B = env.sb([128, 8], F32, "m8")
            wk1, wk1B = env.sb([128, 128], F32, "wk1")
            wk2, wk2B = env.sb([128, 128], F32, "wk2")
            mbf, mbfB = env.sb([128, 128], BF16, "mbf")
            selT, selTB = env.sb([128, 128], BF16, "selT")
            mdg, mdgB = env.sb([128, 128], BF16, "mdg")
            MK, MKB = banks[2]
            retire += [mcmpB, expnB, kselB, vselB, kwinB, vwinB, kcmpB, vcmpB] + [b for _, b in alf.items]
        else:
            kc_, kcB_ = env.sb([68, 2176], BF16, "kc_")
            vc_, vcB_ = env.sb([128, 17, 65], BF16, "vc_")
            esk, eskB = env.sb([128, 8], F32, "esk")
            cx.dma(esk[:], st.sinks, writes=[eskB])
            cx.op("act", lambda e: e.activation(out=esk[:], in_=esk[:], func=AF.Exp), reads=[eskB], writes=[eskB])
            retire += [kcB_, vcB_, eskB]
        gsv = gsig[:].rearrange("p i (h c) -> p i h c", c=3)

        ODbk = banks[4:8]
        ODs, ODsB = env.sb([128, 4, 193], F32, "ODs")
        mb, mbB = env.sb([128, 4, 128], BF16, "mbias")

        def gqa_tile(rhsQ, kap, kB, vap, vB, W, first, last, mask=None, maskB=None, addmask=False):
            S, SB = Sb.next()
            P, PB = Pt.next()
            if addmask:
                cx.op("dve", lambda e: e.tensor_scalar(out=mb[:], in0=mask.unsqueeze(1).to_broadcast([128, 4, 128]), scalar1=-1.0,
                                                       scalar2=65536.0, op0=ALU.add, op1=ALU.mult), reads=[maskB], writes=[mbB])
                cx.op("pe", lambda e: e.matmul(S[:, 0:512], lhsT=kap, rhs=rhsQ, start=True, stop=False), reads=[kB, QB], writes=[SB])
                cx.op("pe", lambda e: e.matmul(S[:, 0:512], lhsT=st.ident[:], rhs=mb[:].rearrange("p a b -> p (a b)"), start=False, stop=True),
                      reads=[st.identB, mbB], writes=[SB])
                mask = None
            else:
                cx.op("pe", lambda e: e.matmul(S[:, 0:512], lhsT=kap, rhs=rhsQ, start=True, stop=True), reads=[kB, QB], writes=[SB])
            if mask is None:
                cx.op("act", lambda e: e.activation(out=P[:].rearrange("p a b -> p (a b)"), in_=S[:, 0:512], func=AF.Exp, scale=0.125),
                      reads=[SB], writes=[PB])
            else:
                E, EB = Et.next()
                cx.op("act", lambda e: e.activation(out=E[:].rearrange("p a b -> p (a b)"), in_=S[:, 0:512], func=AF.Exp, scale=0.125),
                      reads=[SB], writes=[EB])
                cx.op("dve", lambda e: e.tensor_tensor(out=P[:], in0=E[:], in1=mask.unsqueeze(1).to_broadcast([128, 4, 128]), op=ALU.mult),
                      reads=[EB, maskB], writes=[PB])
            for hh in range(4):
                O_, OB_ = ODbk[hh]
                cx.op("pe", lambda e: e.matmul(O_[:, 0:W], lhsT=P[:, hh, :], rhs=vap, start=first, stop=last), reads=[PB, vB], writes=[OB_])
            if last:
                for hh in range(4):
                    O_, OB_ = ODbk[hh]
                    cx.op("act", lambda e: e.activation(out=ODs[:, hh, 0:W], in_=O_[:, 0:W], func=AF.Identity), reads=[OB_], writes=[ODsB])

        for g in range(2):
            if isB:
                cx.dma(ksel[:], st.Ksel[g], writes=[kselB])
                cx.dma(vsel[:], st.Vsel.rearrange("t k g w -> k t g w")[:, :, g, :], writes=[vselB])
                cx.dma(kwin[:], st.Kwin[g], writes=[kwinB])
                cx.dma(vwin[:], st.Vwin.rearrange("t k g w -> k t g w")[:, :, g, :], writes=[vwinB])
                cx.dma(kcmp[:], st.Kcmp[g], writes=[kcmpB])
                cx.dma(vcmp[:], st.Vcmp.rearrange("t k g w -> k t g w")[:, :, g, :], writes=[vcmpB])
            else:
                cx.dma(kc_[:], st.Kc[g], writes=[kcB_])
                cx.dma(vc_[:], st.Vc.rearrange("t k g w -> k t g w")[:, :, g, :], writes=[vcB_])
            yt_, ytB_ = ybT.next()
            for i in range(NQT):
                qo = ((g * 16 + i) * 4) * 128
                rhsQ = Qaug[:, qo:qo + 512]
                if isB:
                    for c_ in range(4):
                        mk = None
                        if c_ == 3:
                            mk = mcmp[:, i, :]
                        elif c_ == 2 and i == 0:
                            mk = mcmp[:, 16, :]
                        gqa_tile(rhsQ, kcmp[:, c_ * 128:(c_ + 1) * 128], kcmpB, vcmp[:, c_, :], vcmpB, 193,
                                 c_ == 0, c_ == 3, mk, mcmpB, addmask=(mk is not None))
                    a_, aB_ = alf.next()
                    cx.dma(a_[:], st.ALF[i], writes=[aB_])
                    cx.op("dve", lambda e: e.tensor_scalar_max(out=dn[:], in0=ODs[:, :, 64], scalar1=1e-30), reads=[ODsB], writes=[dnB])
                    cx.op("dve", lambda e: e.reciprocal(out=dn[:], in_=dn[:]), reads=[dnB], writes=[dnB])
                    cx.op("dve", lambda e: e.tensor_tensor(out=fac[:], in0=dn[:], in1=gsv[:, i, 4 * g:4 * g + 4, 0], op=ALU.mult),
                          reads=[dnB, gsigB], writes=[facB])
                    cx.op("dve", lambda e: e.tensor_tensor(out=ybacc[:], in0=ODs[:, :, 0:64], in1=fac[:].unsqueeze(2).to_broadcast([128, 4, 64]),
                                                           op=ALU.mult), reads=[ODsB, facB], writes=[ybaccB])
                    cx.op("dve", lambda e: e.tensor_scalar(out=impacc[:], in0=ODs[:, 0, 65:193], scalar1=dn[:, 0:1], scalar2=None, op0=ALU.mult),
                          reads=[ODsB, dnB], writes=[impB])
                    for hh in range(1, 4):
                        cx.op("dve", lambda e: e.scalar_tensor_tensor(out=impacc[:], in0=ODs[:, hh, 65:193], scalar=dn[:, hh:hh + 1], in1=impacc[:],
                                                                      op0=ALU.mult, op1=ALU.add), reads=[ODsB, dnB, impB], writes=[impB])
                    cx.op("dve", lambda e: e.tensor_tensor(out=score[:], in0=impacc[:], in1=a_[:, 0, :], op=ALU.mult), reads=[impB, aB_], writes=[scoreB])
                    cx.op("dve", lambda e: e.tensor_tensor(out=score[:], in0=score[:], in1=a_[:, 1, :], op=ALU.add), reads=[scoreB, aB_], writes=[scoreB])
                    cx.op("dve", lambda e: e.max(out=m8[:], in_=score[:]), reads=[scoreB], writes=[m8B])
                    cx.op("dve", lambda e: e.match_replace(out=wk1[:], in_to_replace=m8[:], in_values=score[:], imm_value=-3.0),
                          reads=[scoreB, m8B], writes=[wk1B])
                    cx.op("dve", lambda e: e.max(out=m8[:], in_=wk1[:]), reads=[wk1B], writes=[m8B])
                    cx.op("dve", lambda e: e.match_replace(out=wk2[:], in_to_replace=m8[:], in_values=wk1[:], imm_value=-3.0),
                          reads=[wk1B, m8B], writes=[wk2B])
                    cx.op("dve", lambda e: e.tensor_tensor(out=wk1[:], in0=score[:], in1=wk2[:], op=ALU.subtract), reads=[scoreB, wk2B], writes=[wk1B])
                    cx.op("dve", lambda e: e.scalar_tensor_tensor(out=mbf[:], in0=wk1[:], scalar=1.0, in1=a_[:, 0, :], op0=ALU.min, op1=ALU.mult),
                          reads=[wk1B, aB_], writes=[mbfB])
                    cx.op("pe", lambda e: e.transpose(TRb[:, 0:128], mbf[:], st.ident[:]), reads=[mbfB, st.identB], writes=[TRB])
                    cx.op("act", lambda e: e.activation(out=selT[:], in_=TRb[:, 0:128], func=AF.Identity), reads=[TRB], writes=[selTB])
                    def gate_epilogue(ci):
                        cx.op("dve", lambda e: e.tensor_scalar_max(out=dn[:], in0=ODs[:, :, 64], scalar1=1e-30), reads=[ODsB], writes=[dnB])
                        cx.op("dve", lambda e: e.reciprocal(out=dn[:], in_=dn[:]), reads=[dnB], writes=[dnB])
                        cx.op("dve", lambda e: e.tensor_tensor(out=fac[:], in0=dn[:], in1=gsv[:, i, 4 * g:4 * g + 4, ci], op=ALU.mult),
                              reads=[dnB, gsigB], writes=[facB])
                        cx.op("dve", lambda e: e.tensor_tensor(out=ytmp[:], in0=ODs[:, :, 0:64], in1=fac[:].unsqueeze(2).to_broadcast([128, 4, 64]),
                                                               op=ALU.mult), reads=[ODsB, facB], writes=[ytmpB])
                        cx.op("pool", lambda e: e.tensor_tensor(out=ybacc[:], in0=ybacc[:], in1=ytmp[:], op=ALU.add),
                              reads=[ybaccB, ytmpB], writes=[ybaccB])
                    nkt = SEL_T0 + i + 1
                    for kt in range(nkt):
                        sl = (kt % 4) * 128
                        cx.op("pe", lambda e: e.matmul(MK[:, sl:sl + 128], lhsT=expn[:, kt * 128:(kt + 1) * 128], rhs=selT[:], start=True, stop=True),
                              reads=[expnB, selTB], writes=[MKB])
                        if kt == nkt - 1:
                            cx.op("dve", lambda e: e.tensor_tensor(out=mdg[:], in0=MK[:, sl:sl + 128], in1=caus[:, 0, :], op=ALU.mult),
                                  reads=[MKB, causB], writes=[mdgB])
                            mk, mkB = mdg[:], mdgB
                        else:
                            mk, mkB = MK[:, sl:sl + 128], MKB
                        gqa_tile(rhsQ, ksel[:, kt * 128:(kt + 1) * 128], kselB, vsel[:, kt, :], vselB, 65, kt == 0, kt == nkt - 1, mk, mkB)
                    gate_epilogue(1)
                    for d in range(4, -1, -1):
                        wt = 4 + i - d
                        mk = caus[:, 1, :] if d == 4 else (caus[:, 0, :] if d == 0 else None)
                        gqa_tile(rhsQ, kwin[:, wt * 128:(wt + 1) * 128], kwinB, vwin[:, wt, :], vwinB, 65, d == 4, d == 0, mk, causB)
                    gate_epilogue(2)
                else:
                    for d in (1, 0):
                        wt = 1 + i - d
                        mk = caus[:, 1, :] if d == 1 else caus[:, 0, :]
                        gqa_tile(rhsQ, kc_[:, wt * 128:(wt + 1) * 128], kcB_, vc_[:, wt, :], vcB_, 65, d == 1, d == 0, mk, causB)
                    cx.op("dve", lambda e: e.tensor_tensor(out=dn[:], in0=ODs[:, :, 64], in1=esk[:, 4 * g:4 * g + 4], op=ALU.add),
                          reads=[ODsB, eskB], writes=[dnB])
                    cx.op("dve", lambda e: e.reciprocal(out=dn[:], in_=dn[:]), reads=[dnB], writes=[dnB])
                    cx.op("dve", lambda e: e.tensor_tensor(out=ybacc[:], in0=ODs[:, :, 0:64], in1=dn[:].unsqueeze(2).to_broadcast([128, 4, 64]),
                                                           op=ALU.mult), reads=[ODsB, dnB], writes=[ybaccB])
                cx.op("pool", lambda e: e.tensor_tensor(out=ybf[:], in0=ybacc[:].rearrange("p h d -> p (h d)"),
                                                        in1=gS[:, i, g * 256:(g + 1) * 256], op=ALU.mult), reads=[ybaccB, gSB], writes=[ybfB])
                for cc in range(2):
                    cx.op("pe", lambda e: e.transpose(TRb[:, 256 + cc * 128:256 + (cc + 1) * 128], ybf[:, cc * 128:(cc + 1) * 128], st.ident[:]),
                          reads=[ybfB, st.identB], writes=[TRB])
                cx.op("act", lambda e: e.activation(out=yt_[:, :, i * 128:(i + 1) * 128],
                                                    in_=TRb[:, 256:512].rearrange("p (c q) -> p c q", c=2), func=AF.Identity),
                      reads=[TRB], writes=[ytB_])
            for cc in range(2):
                cx.dma(st.yT[ybr, 2 * g + cc], yt_[:, cc, :], reads=[ytB_], writes=[st.yB[ybr][2 * g + cc]], queue="pool", owner=ytB_)
        cx.barrier()
        cx.retire(retire + [b for _, b in ybT.items])
        env.es = st.es


def phase_final(st):
    cx, env, banks = st.cx, st.env, st.banks
    with ExitStack() as ph:
        env.es = ph
        bM_s, bMB = env.sb([128, 24], F32, "bM_s")
        cx.dma(bM_s[:], st.bM, writes=[bMB])
        lg, lgB = env.sb([128, 8], F32, "lg")
        lb, lbB = env.sb([128, 8], F32, "lb")
        cx.dma(lg[:], st.lng, writes=[lgB])
        cx.dma(lb[:], st.lnb, writes=[lbB])
        stg, stgB = env.sb([128, 4, 1024], F32, "wstg")
        wbr, wbrB = env.sb([128, 3, 4, 1024], BF16, "wbr")
        wo, woB = env.sb([128, 8, 1024], BF16, "wo")
        for br in range(3):
            cx.dma(stg[:], st.wBr[br], writes=[stgB])
            for h_ in range(4):
                eng = "pool" if h_ % 2 == 0 else "dve"
                cx.op(eng, lambda e: e.tensor_copy(out=wbr[:, br, h_, :], in_=stg[:, h_, :]), reads=[stgB], writes=[wbrB])
        for half in range(2):
            cx.dma(stg[:], st.wO[:, half * 4:(half + 1) * 4, :], writes=[stgB])
            for h_ in range(4):
                eng = "pool" if h_ % 2 == 0 else "dve"
                cx.op(eng, lambda e: e.tensor_copy(out=wo[:, half * 4 + h_, :], in_=stg[:, h_, :]), reads=[stgB], writes=[woB])
        x32, x32B = env.sb([128, 8, 512], F32, "x32")
        xbt, xbtB = env.sb([128, 8, 512], BF16, "xbt")
        yt, ytB = env.sb([128, 3, 4, 512], BF16, "yt")
        sg, sgB = env.sb([128, 512], F32, "sg")
        mrg, mrgB = env.sb([128, 8, 512], F32, "mrg")
        mrb, mrbB = env.sb([128, 8, 512], BF16, "mrb")
        z, zB = env.sb([128, 8, 512], F32, "z")
        zsq, zsqB = env.sb([128, 8, 512], F32, "zsq")
        mean, meanB = env.sb([128, 512], F32, "mean")
        rstd, rstdB = env.sb([128, 512], F32, "rstd")
        tmp, tmpB = env.sb([128, 512], F32, "lntmp")
        fm = FMProj(env, xbt, xbtB, banks[0:2])
        Zb = Rot(banks[2:4])
        Yb = Rot(banks[4:6])
        (S1, S1B), (S2, S2B) = banks[6], banks[7]
        yrd = [b for row in st.yB for b in row]
        for tt in range(4):
            tsl = slice(tt * 512, (tt + 1) * 512)
            cx.dma(x32[:], st.xT.rearrange("k p t -> p k t")[:, :, tsl], writes=[x32B])
            for kc in range(8):
                eng = "pool" if kc % 2 == 0 else "dve"
                cx.op(eng, lambda e: e.tensor_copy(out=xbt[:, kc, :], in_=x32[:, kc, :]), reads=[x32B], writes=[xbtB])
            cx.dma(yt[:], st.yT.rearrange("b c p t -> p b c t")[:, :, :, tsl], reads=yrd, writes=[ytB])

            def consume_m(tag, ti, t0, nt, p, pB):
                br, dc = divmod(tag, 8)
                cx.op("act", lambda e: e.activation(out=sg[:], in_=p[:, 0:512], func=AF.Sigmoid, bias=bM_s[:, tag:tag + 1], scale=1.0),
                      reads=[pB, bMB], writes=[sgB])
                Zp, ZpB = Zb.next()
                for cc in range(4):
                    cx.op("pe", lambda e: e.matmul(Zp[:, 0:512], lhsT=wbr[:, br, cc, dc * 128:(dc + 1) * 128], rhs=yt[:, br, cc, :],
                                                   start=(cc == 0), stop=(cc == 3)), reads=[wbrB, ytB], writes=[ZpB])
                if br == 0:
                    cx.op("dve", lambda e: e.tensor_tensor(out=mrg[:, dc, :], in0=Zp[:, 0:512], in1=sg[:], op=ALU.mult),
                          reads=[ZpB, sgB], writes=[mrgB])
                else:
                    cx.op("dve", lambda e: e.tensor_tensor(out=tmp[:], in0=Zp[:, 0:512], in1=sg[:], op=ALU.mult),
                          reads=[ZpB, sgB], writes=[tmpB])
                    o_ = mrb[:, dc, :] if br == 2 else mrg[:, dc, :]
                    cx.op("pool", lambda e: e.tensor_tensor(out=o_, in0=mrg[:, dc, :], in1=tmp[:], op=ALU.add),
                          reads=[mrgB, tmpB], writes=[mrbB if br == 2 else mrgB])
            fm.run([(st.wM[c], 128, c) for c in range(24)], [(0, 512)], consume_m)
            for dc in range(8):
                Yp, YpB = Yb.next()
                for kc in range(8):
                    cx.op("pe", lambda e: e.matmul(Yp[:, 0:512], lhsT=wo[:, kc, dc * 128:(dc + 1) * 128], rhs=mrb[:, kc, :],
                                                   start=(kc == 0), stop=(kc == 7)), reads=[woB, mrbB], writes=[YpB])
                cx.op("dve", lambda e: e.scalar_tensor_tensor(out=z[:, dc, :], in0=x32[:, dc, :], scalar=float(ALPHA), in1=Yp[:, 0:512],
                                                              op0=ALU.mult, op1=ALU.add), reads=[x32B, YpB], writes=[zB])
                cx.op("pool", lambda e: e.tensor_tensor(out=zsq[:, dc, :], in0=z[:, dc, :], in1=z[:, dc, :], op=ALU.mult),
                      reads=[zB], writes=[zsqB])
            for dc in range(8):
                cx.op("pe", lambda e: e.matmul(S1[:, 0:512], lhsT=st.ones_f[:], rhs=z[:, dc, :], start=(dc == 0), stop=(dc == 7)),
                      reads=[st.onesfB, zB], writes=[S1B])
            for dc in range(8):
                cx.op("pe", lambda e: e.matmul(S2[:, 0:512], lhsT=st.ones_f[:], rhs=zsq[:, dc, :], start=(dc == 0), stop=(dc == 7)),
                      reads=[st.onesfB, zsqB], writes=[S2B])
            cx.op("act", lambda e: e.activation(out=mean[:], in_=S1[:, 0:512], func=AF.Identity, scale=1.0 / D), reads=[S1B], writes=[meanB])
            cx.op("dve", lambda e: e.tensor_tensor(out=tmp[:], in0=mean[:], in1=mean[:], op=ALU.mult), reads=[meanB], writes=[tmpB])
            cx.op("dve", lambda e: e.scalar_tensor_tensor(out=rstd[:], in0=S2[:, 0:512], scalar=1.0 / D, in1=tmp[:], op0=ALU.mult, op1=ALU.subtract),
                  reads=[S2B, tmpB], writes=[rstdB])
            cx.op("dve", lambda e: e.tensor_scalar(out=rstd[:], in0=rstd[:], scalar1=float(LN_EPS), scalar2=None, op0=ALU.add),
                  reads=[rstdB], writes=[rstdB])
            cx.op("act", lambda e: e.activation(out=rstd[:], in_=rstd[:], func=AF.Sqrt), reads=[rstdB], writes=[rstdB])
            cx.op("dve", lambda e: e.reciprocal(out=rstd[:], in_=rstd[:]), reads=[rstdB], writes=[rstdB])
            for dc in range(8):
                cx.op("dve", lambda e: e.tensor_tensor(out=zsq[:, dc, :], in0=z[:, dc, :], in1=mean[:], op=ALU.subtract),
                      reads=[zB, meanB], writes=[zsqB])
                cx.op("pool", lambda e: e.tensor_tensor(out=zsq[:, dc, :], in0=zsq[:, dc, :], in1=rstd[:], op=ALU.mult),
                      reads=[zsqB, rstdB], writes=[zsqB])
                cx.op("dve", lambda e: e.tensor_scalar(out=z[:, dc, :], in0=zsq[:, dc, :], scalar1=lg[:, dc:dc + 1], scalar2=lb[:, dc:dc + 1],
                                                       op0=ALU.mult, op1=ALU.add), reads=[zsqB, lgB, lbB], writes=[zB])
            ob = Buf("xn%d" % tt)
            st.outs.append(ob)
            cx.dma(st.xnT.rearrange("k p t -> p k t")[:, :, tsl], z[:], reads=[zB], writes=[ob], queue="pool", owner=zB)
        cx.barrier()
        env.es = st.es


def p2_weights(w_in_l, b_in_l, w_branch_l, w_out_l, ln_g_l, ln_b_l, sinks_l):
    ar = np.arange
    colsA = [O_AQ + 128 * c + ar(128) for c in range(12)] + [O_AG + 128 * c + ar(128) for c in range(4)]
    wA = np.stack([_fm_tile(w_in_l, c) for c in colsA])
    bA = np.ascontiguousarray(np.stack([b_in_l[c] for c in colsA], axis=1))
    colsQ = [O_BQ + 64 * h + ar(64) for h in range(8)] + [O_CQ + 64 * h + ar(64) for h in range(8)]
    wQ = np.stack([_fm_tile(w_in_l, c) for c in colsQ])
    bQ = np.ascontiguousarray(np.stack([b_in_l[c] for c in colsQ], axis=1))
    wT = np.zeros((3, 128, 8, 512), np.float32)
    bT = np.zeros((3, 128, 512), np.float32)
    for i, c in enumerate([O_BG + ar(512), O_BGATE + ar(24), O_CG + ar(512)]):
        wT[i, :, :, :len(c)] = _fm_tile(w_in_l, c)
        bT[i, :, :len(c)] = b_in_l[c][None, :]
    colsM = [O_MG + 128 * c + ar(128) for c in range(24)]
    wM = np.stack([_fm_tile(w_in_l, c) for c in colsM])
    bM = np.ascontiguousarray(np.stack([b_in_l[c] for c in colsM], axis=1))
    wBr = np.ascontiguousarray(w_branch_l.reshape(3, 4, 128, 1024).transpose(0, 2, 1, 3))
    wO = np.ascontiguousarray(w_out_l.reshape(8, 128, 1024).transpose(1, 0, 2))
    lng = np.ascontiguousarray(ln_g_l.reshape(8, 128).T)
    lnb = np.ascontiguousarray(ln_b_l.reshape(8, 128).T)
    sinks = np.ascontiguousarray(np.broadcast_to(sinks_l[None, :], (128, 8))).astype(np.float32)
    return dict(wA=wA, bA=bA, wQ=wQ, bQ=bQ, wT=wT, bT=bT, wM=wM, bM=bM, wBr=wBr, wO=wO, lng=lng, lnb=lnb, sinks=sinks)


def _aug_rows(pos):
    pos = np.asarray(pos, np.int64)
    return np.stack([(pos // 128) * 128, pos % 128, np.ones_like(pos), np.ones_like(pos)]).astype(np.float32)


_CONST_CACHE = {}


def p2_consts(j):
    if j in _CONST_CACHE:
        return _CONST_CACHE[j]
    kq = np.arange(128)
    k_, q_ = kq[:, None], kq[None, :]
    out = {}
    out["ident"] = np.eye(128, dtype=np.float32)
    slopes_a = 2.0 ** (-8.0 * np.arange(1, 25) / 24.0)
    TBA = np.zeros((4, 128, 3, 2, 4, 128), np.float32)
    for g in range(3):
        for pr in range(4):
            for hh in range(2):
                sl = slopes_a[g * 8 + pr * 2 + hh] * A_DIL[g]
                diag = np.where(k_ <= q_, np.exp(-sl * (q_ - k_)), 0.0)
                prev = np.where(k_ >= q_, np.exp(-sl * (q_ - k_ + 128)), 0.0)
                TBA[pr, :, g, hh, 0] = prev if j > 0 else 0.0
                TBA[pr, :, g, hh, 1] = diag
                TBA[pr, :, g, hh, 2] = prev
                TBA[pr, :, g, hh, 3] = diag
    out["TBA"] = TBA.reshape(4, 128, -1)
    QA = np.zeros((4, 2, 16, 4, 128), np.float32)
    for g in range(2):
        for hh in range(4):
            s8 = 8.0 * 2.0 ** (-(4 * g + hh + 1))
            for i in range(16):
                QA[0, g, i, hh] = s8
                QA[1, g, i, hh] = s8
                QA[2, g, i, hh] = -s8 * 128 * (SEL_T0 + i)
                QA[3, g, i, hh] = -s8 * kq
    out["QAUG"] = QA.reshape(4, -1).astype(NPBF)
    M = np.zeros((128, 17, 128), np.float32)
    for i in range(16):
        M[:, i, :] = (128 * i + q_ >= 16 * k_ + 31)
    M[:, 16, :] = (q_ >= 16 * (k_ - 128) + 31)
    out["Mcmp"] = M.astype(NPBF)
    C = np.zeros((128, 2, 128), np.float32)
    C[:, 0, :] = (k_ <= q_)
    C[:, 1, :] = (k_ >= q_ + 1)
    out["CAUS"] = C.astype(NPBF)
    out["EXP"] = (np.arange(128)[:, None] == (np.arange(8192)[None, :] // 64)).astype(NPBF)
    ALF = np.zeros((16, 128, 2, 128), np.float32)
    s0 = 32 * (3 - j)
    s_ = np.arange(128)[None, :]
    for i in range(16):
        cur = (96 + 2 * i + (kq >= 64))[:, None]
        al = ((s_ >= s0) & (s_ <= cur)).astype(np.float32)
        bonus = np.zeros((128, 128), np.float32)
        bonus = np.maximum(bonus, np.where(s_ == s0, 16384.0, 0.0))
        bonus = np.maximum(bonus, np.where((s_ == cur - 1) & (cur - 1 >= s0), 65536.0, 0.0))
        bonus = np.maximum(bonus, np.where(s_ == cur, 32768.0, 0.0))
        ALF[i, :, 0, :] = al
        ALF[i, :, 1, :] = (al - 1.0) + bonus
    out["ALF"] = ALF
    _CONST_CACHE[j] = out
    return out


def _win(arr, axis, lo, hi):
    n = arr.shape[axis]
    pad_lo = max(0, -lo)
    sl = [slice(None)] * arr.ndim
    sl[axis] = slice(max(lo, 0), min(hi, n))
    seg = arr[tuple(sl)]
    if pad_lo or hi > n:
        pw = [(0, 0)] * arr.ndim
        pw[axis] = (pad_lo, max(0, hi - n))
        seg = np.pad(seg, pw)
    return seg


def p2_kv_inputs(p1res, b):
    cores = [p1res[b * 4 + j] for j in range(4)]
    KTb = np.concatenate([np.asarray(c["KT"]) for c in cores], axis=2)
    Vb = np.concatenate([np.asarray(c["V"]) for c in cores], axis=0)
    kcb = np.concatenate([np.asarray(c["kcT"]) for c in cores], axis=1)
    vcb = np.concatenate([np.asarray(c["vc"]) for c in cores], axis=0)
    one = np.ones((), NPBF)
    ovl = np.zeros((512, 128), NPBF)
    jr = np.arange(512)
    ovl[jr, jr // 4] = 1
    m3 = jr[(jr % 4 == 3) & (jr // 4 + 1 < 128)]
    ovl[m3, m3 // 4 + 1] = 1
    res = []
    for j in range(4):
        d = {}
        for g in range(3):
            r = A_DIL[g]
            M = 128 + T // r
            lo, hi = T * j - 128 * r, T * (j + 1)
            W = _win(KTb[g * 4:(g + 1) * 4], 2, lo, hi)
            d["KA%d" % g] = np.ascontiguousarray(W.reshape(4, 128, M, r).transpose(0, 1, 3, 2).reshape(4, 128, r * M))
            Wv = _win(Vb[:, g * 512:(g + 1) * 512], 0, lo, hi)
            d["VA%d" % g] = np.ascontiguousarray(Wv.reshape(M, r, 512).transpose(1, 0, 2).reshape(r * M // 128, 128, 512))
        end = T * (j + 1)

        def kaug(kt_rows, lo, hi, pos):
            n = hi - lo
            out = np.zeros((2, 68, n), NPBF)
            w = _win(kt_rows, 1, lo, hi)
            out[:, 0:64, :] = w.reshape(2, 64, n)
            out[:, 64:68, :] = _aug_rows(pos).astype(NPBF)[None]
            return out

        def vaug(v_rows, lo, hi):
            n = hi - lo
            w = _win(v_rows, 0, lo, hi).reshape(n, 2, 64)
            valid = ((np.arange(lo, hi) >= 0) & (np.arange(lo, hi) < SEQ)).astype(NPBF)
            out = np.zeros((n, 2, 65), NPBF)
            out[:, :, 0:64] = w
            out[:, :, 64] = valid[:, None]
            return out.reshape(n // 128, 128, 2, 65)

        d["Ksel"] = kaug(KTb[12], end - SEQ, end, np.arange(SEQ))
        d["Vsel"] = vaug(Vb[:, 1536:1664], end - SEQ, end)
        d["Kwin"] = kaug(KTb[13], T * j - 512, end, 6144 - 512 + np.arange(2560))
        d["Vwin"] = vaug(Vb[:, 1664:1792], T * j - 512, end)
        d["Kc"] = kaug(KTb[14], T * j - 128, end, 6144 - 128 + np.arange(2176))
        d["Vc"] = vaug(Vb[:, 1792:1920], T * j - 128, end)
        blo, bhi = 128 * (j + 1) - 512, 128 * (j + 1)
        d["Kcmp"] = kaug(kcb, blo, bhi, 16 * np.arange(512) + 31)
        vw = _win(vcb, 0, blo, bhi).reshape(512, 2, 64)
        ab = np.arange(blo, bhi)
        valid = ((ab >= 0) & (ab < 511)).astype(NPBF)
        vcm = np.zeros((512, 2, 193), NPBF)
        vcm[:, :, 0:64] = vw * valid[:, None, None]
        vcm[:, :, 64] = valid[:, None]
        vcm[:, :, 65:193] = (ovl * valid[:, None])[:, None, :]
        d["Vcmp"] = vcm.reshape(4, 128, 2, 193)
        res.append(d)
    return res


def run_p2(x_full, wts, p1res, stop_after=None):
    nc = get_nc("p2" + str(stop_after), lambda: build_p2(stop_after))
    in_maps = []
    kv = [p2_kv_inputs(p1res, b) for b in range(NBATCH)]
    for c in range(NCORE):
        b, j = divmod(c, 4)
        m = dict(wts)
        m.update(p2_consts(j))
        m.update(kv[b][j])
        m["xT"] = xT_for_core(x_full[b], j, 0)
        in_maps.append(m)
    import time as _t
    _t0 = _t.time()
    res = run_bass_kernel_spmd(nc, in_maps, core_ids=list(range(NCORE)))
    print("[p2] device call %.1fs" % (_t.time() - _t0), flush=True)
    return res.results


def unshard_xT(results, key="xnT"):
    out = np.zeros((NBATCH, SEQ, D), np.float32)
    for c in range(NCORE):
        b, j = divmod(c, 4)
        out[b, j * T:(j + 1) * T] = np.asarray(results[c][key]).reshape(D, T).T
    return out


def kernel(x, w_in, b_in, w_cmp1, w_cmp2, cmp_pos, sinks, w_branch, w_out, ln_g, ln_b):
    x = np.ascontiguousarray(np.asarray(x, np.float32))
    f = lambda a: np.asarray(a, np.float32)
    for l in range(DEPTH):
        w1 = p1_weights(f(w_in[l]), f(b_in[l]), f(w_cmp1[l]), f(w_cmp2[l]), f(cmp_pos[l]))
        p1res = run_p1(x, w1)
        w2 = p2_weights(f(w_in[l]), f(b_in[l]), f(w_branch[l]), f(w_out[l]), f(ln_g[l]), f(ln_b[l]), f(sinks[l]))
        p2res = run_p2(x, w2, p1res)
        x = unshard_xT(p2res)
    return x
```

```python
import numpy as np
import ml_dtypes
from contextlib import ExitStack
import concourse.bass as bass
import concourse.mybir as mybir
from concourse.bass_utils import run_bass_kernel_spmd

F32 = mybir.dt.float32
BF16 = mybir.dt.bfloat16
AF = mybir.ActivationFunctionType
ALU = mybir.AluOpType
NPBF = ml_dtypes.bfloat16

D = 1024
SEQ = 8192
NBATCH = 2
DEPTH = 4
T = 2048
NQT = 16
NCORE = 8
ALPHA = (2 * DEPTH) ** 0.25
LN_EPS = 1e-5
O_AQ, O_AK, O_AV, O_AG = 0, 1536, 3072, 4608
O_BQ, O_BCK, O_BCV, O_BSK, O_BSV, O_BWK, O_BWV, O_BG, O_BGATE = 5120, 5632, 5760, 5888, 6016, 6144, 6272, 6400, 6912
O_CQ, O_CK, O_CV, O_CG, O_MG = 6936, 7448, 7576, 7704, 8216
A_DIL = (1, 4, 16)


class Buf:
    __slots__ = ("name", "w", "r", "sem", "ndma")

    def __init__(self, name):
        self.name = name
        self.w = None
        self.r = {}
        self.sem = None
        self.ndma = 0


class Ctx:
    def __init__(self, nc, es, n_dma_sems=100):
        self.nc = nc
        self.es = es
        self.sems = []
        self.engs = {}
        for name, eng in (("pe", nc.tensor), ("act", nc.scalar), ("dve", nc.vector),
                          ("pool", nc.gpsimd), ("sp", nc.sync)):
            k = self._newsem(name + "_sem")
            self.engs[name] = dict(eng=eng, sem=k, cnt=0, seen={})
        self.free_dma = []
        self.dma_latest = {}
        self.nwaits = 0
        self.nops = 0

    def _newsem(self, name):
        h = self.es.enter_context(self.nc.semaphore(name))
        self.sems.append(h)
        return len(self.sems) - 1

    def _deps(self, reads, writes):
        deps = {}
        for b in reads:
            if b.w is not None:
                deps[b.w[0]] = max(deps.get(b.w[0], 0), b.w[1])
        for b in writes:
            if b.w is not None:
                deps[b.w[0]] = max(deps.get(b.w[0], 0), b.w[1])
            for k, v in b.r.items():
                deps[k] = max(deps.get(k, 0), v)
        return deps

    def _wait(self, e, deps):
        for k, v in deps.items():
            if e["seen"].get(k, 0) >= v:
                continue
            e["eng"].wait_ge(self.sems[k], v)
            e["seen"][k] = v
            self.nwaits += 1

    def _record(self, rec, reads, writes):
        for b in reads:
            b.r[rec[0]] = max(b.r.get(rec[0], 0), rec[1])
        for b in writes:
            b.w = rec
            b.r = {}

    def op(self, ename, fn, reads=(), writes=()):
        e = self.engs[ename]
        deps = self._deps(reads, writes)
        if ename == "pe":
            deps.pop(e["sem"], None)
        self._wait(e, deps)
        ins = fn(e["eng"])
        ins.then_inc(self.sems[e["sem"]], 1)
        e["cnt"] += 1
        self.nops += 1
        self._record((e["sem"], e["cnt"]), reads, writes)
        return ins

    def dma(self, out, in_, reads=(), writes=(), queue="sp", owner=None, **kw):
        e = self.engs[queue]
        if owner is None:
            owner = (list(writes) + list(reads))[0]
        if owner.sem is None:
            if self.free_dma:
                owner.sem, owner.ndma = self.free_dma.pop()
            else:
                owner.sem = self._newsem("dma%d" % len(self.sems))
        deps = self._deps(reads, writes)
        self._wait(e, deps)
        ins = e["eng"].dma_start(out=out, in_=in_, **kw)
        ins.then_inc(self.sems[owner.sem], 16)
        owner.ndma += 1
        self.nops += 1
        self._record((owner.sem, 16 * owner.ndma), reads, writes)
        self.dma_latest[owner.sem] = 16 * owner.ndma
        return ins

    def barrier(self):
        deps = dict(self.dma_latest)
        for e in self.engs.values():
            if e["cnt"]:
                deps[e["sem"]] = e["cnt"]
        for e in self.engs.values():
            self._wait(e, deps)

    def retire(self, bufs):
        for b in bufs:
            if b.sem is not None:
                self.free_dma.append((b.sem, b.ndma))
                b.sem = None

    def wait_all(self, ename, bufs):
        e = self.engs[ename]
        self._wait(e, self._deps((), bufs))


class Rot:
    def __init__(self, items):
        self.items = items
        self.i = 0

    def next(self):
        it = self.items[self.i % len(self.items)]
        self.i += 1
        return it


class Env:
    def __init__(self, nc, es, cx):
        self.nc, self.es, self.cx = nc, es, cx
        self.n = 0

    def sb(self, shape, dt, name=None):
        self.n += 1
        name = "%s_s%d" % (name or "t", self.n)
        t = self.es.enter_context(self.nc.sbuf_tensor(name, shape, dt))
        return t, Buf(name)

    def ps(self, name):
        t = self.es.enter_context(self.nc.psum_tensor(name, [128, 512], F32))
        return t, Buf(name)

    def din(self, name, shape, dt):
        return self.nc.dram_tensor(name, list(shape), dt, kind="ExternalInput").ap()

    def dout(self, name, shape, dt):
        return self.nc.dram_tensor(name, list(shape), dt, kind="ExternalOutput").ap(), Buf(name)


def load_xT(env, xT, ntok, xb, xbB):
    cx = env.cx
    stg = Rot([env.sb([128, ntok], F32) for _ in range(2)])
    for kc in range(8):
        s, sB = stg.next()
        cx.dma(s[:], xT[kc], writes=[sB])
        eng = "dve" if kc % 2 == 0 else "pool"
        cx.op(eng, lambda e: e.tensor_copy(out=xb[:, kc, :], in_=s[:]), reads=[sB], writes=[xbB])


class FMProj:
    def __init__(self, env, xb, xbB, psbanks, mmax=128, cast_engs=("pool", "pool", "dve")):
        self.cast_engs = cast_engs
        self.env, self.xb, self.xbB = env, xb, xbB
        self.wf = Rot([env.sb([128, 8, mmax], F32) for _ in range(3)])
        self.wb = Rot([env.sb([128, 8, mmax], BF16) for _ in range(3)])
        self.ps = Rot(psbanks)
        self.ncast = 0

    def run(self, chunks, tiles, consume):
        cx = self.env.cx
        loaded = []

        def load(j):
            ap, m, tag = chunks[j]
            f, fB = self.wf.next()
            cx.dma(f[:, :, 0:m], ap, writes=[fB])
            loaded.append((f, fB))

        def cast(j):
            ap, m, tag = chunks[j]
            f, fB = loaded[j]
            b, bB = self.wb.next()
            eng = self.cast_engs[self.ncast % len(self.cast_engs)]
            self.ncast += 1
            if eng == "act":
                cx.op(eng, lambda e: e.activation(out=b[:, :, 0:m], in_=f[:, :, 0:m], func=AF.Identity), reads=[fB], writes=[bB])
            else:
                cx.op(eng, lambda e: e.tensor_copy(out=b[:, :, 0:m], in_=f[:, :, 0:m]), reads=[fB], writes=[bB])
            return b, bB

        n = len(chunks)
        casted = {}
        for j in range(min(2, n)):
            load(j)
        casted[0] = cast(0)
        for j in range(n):
            if j + 2 < n:
                load(j + 2)
            if j + 1 < n:
                casted[j + 1] = cast(j + 1)
            ap, m, tag = chunks[j]
            b, bB = casted.pop(j)
            for ti, (t0, nt) in enumerate(tiles):
                p, pB = self.ps.next()
                for kc in range(8):
                    cx.op("pe", lambda e: e.matmul(p[0:m, 0:nt], lhsT=b[:, kc, 0:m], rhs=self.xb[:, kc, t0:t0 + nt],
                                                   start=(kc == 0), stop=(kc == 7)),
                          reads=[bB, self.xbB], writes=[pB])
                consume(tag, ti, t0, nt, p, pB)


class TMProj:
    def __init__(self, env, xb, xbB, psbanks):
        self.env, self.xb, self.xbB = env, xb, xbB
        self.wf, self.wfB = env.sb([128, 8, 512], F32)
        self.wb, self.wbB = env.sb([128, 8, 512], BF16)
        self.ps = Rot(psbanks)

    def run(self, w_ap, ncols, ntiles, consume):
        cx = self.env.cx
        cx.dma(self.wf[:, :, 0:ncols], w_ap, writes=[self.wfB])
        for kc in range(8):
            eng = "pool" if kc % 2 == 0 else "dve"
            cx.op(eng, lambda e: e.tensor_copy(out=self.wb[:, kc, 0:ncols], in_=self.wf[:, kc, 0:ncols]),
                  reads=[self.wfB], writes=[self.wbB])
        for i in range(ntiles):
            p, pB = self.ps.next()
            for kc in range(8):
                cx.op("pe", lambda e: e.matmul(p[:, 0:ncols], lhsT=self.xb[:, kc, i * 128:(i + 1) * 128],
                                               rhs=self.wb[:, kc, 0:ncols], start=(kc == 0), stop=(kc == 7)),
                      reads=[self.wbB, self.xbB], writes=[pB])
            consume(i, p, pB)


P1_NFM = 17
P1_NTM = 4
P1_TMW = (512, 512, 512, 384)


def build_p1():
    nc = bass.Bass("TRN2", target_bir_lowering=False)
    es = ExitStack()
    with es:
        cx = Ctx(nc, es)
        env = Env(nc, es, cx)
        xT = env.din("xT", [8, 128, T + 16], F32)
        wfm = env.din("wfm", [P1_NFM, 128, 8, 128], F32)
        bfm = env.din("bfm", [128, P1_NFM], F32)
        wtm = env.din("wtm", [P1_NTM, 128, 8, 512], F32)
        btm = env.din("btm", [P1_NTM, 128, 512], F32)
        w1 = env.din("w1", [2, 64, 32, 256], F32)
        w2 = env.din("w2", [2, 128, 2, 64], F32)
        pos = env.din("pos", [2, 128, 32], F32)
        KT, KTB = env.dout("KT", [15, 128, T], BF16)
        V, VB = env.dout("V", [T, 1920], BF16)
        kcT, kcTB = env.dout("kcT", [128, 128], BF16)
        vc, vcB = env.dout("vc", [128, 128], BF16)

        outs = []
        xb, xbB = env.sb([128, 8, T + 16], BF16, "xb")
        load_xT(env, xT, T + 16, xb, xbB)
        bfm_s, bfmB = env.sb([128, P1_NFM], F32, "bfm_s")
        cx.dma(bfm_s[:], bfm, writes=[bfmB])
        banks = [env.ps("pa%d" % i) for i in range(4)]
        pcs = [env.ps("pc0"), env.ps("pc1")]
        pd, pdB = env.ps("pd")

        fm = FMProj(env, xb, xbB, banks)
        kts = Rot([env.sb([128, T], BF16) for _ in range(2)])
        bc, bcB = env.sb([128, 2, T + 16], BF16, "bc")
        cur = {}

        def consume_fm(tag, ti, t0, nt, p, pB):
            if tag < 15:
                if ti == 0:
                    cur["kt"] = kts.next()
                k, kB = cur["kt"]
                cx.op("act", lambda e: e.activation(out=k[:, t0:t0 + nt], in_=p[:, 0:nt], func=AF.Identity,
                                                    bias=bfm_s[:, tag:tag + 1], scale=1.0),
                      reads=[pB, bfmB], writes=[kB])
                if ti == 3:
                    ob = Buf("o"); outs.append(ob)
                    cx.dma(KT[tag], k[:], reads=[kB], writes=[ob], queue="pool", owner=kB)
            else:
                cx.op("act", lambda e: e.activation(out=bc[:, tag - 15, t0:t0 + nt], in_=p[:, 0:nt], func=AF.Identity,
                                                    bias=bfm_s[:, tag:tag + 1], scale=1.0),
                      reads=[pB, bfmB], writes=[bcB])

        tiles4 = [(i * 512, 512) for i in range(4)]
        fm.run([(wfm[c], 128, c) for c in range(15)], tiles4, consume_fm)
        fm.run([(wfm[c], 128, c) for c in (15, 16)], tiles4 + [(T, 16)], consume_fm)

        STAGE = 3
        tm = TMProj(env, xb, xbB, banks)
        bt_s, btB = env.sb([128, 512], F32, "bt_s")
        vs = Rot([env.sb([128, 512], BF16) for _ in range(3)])
        coff = 0
        for gi in range(P1_NTM if STAGE >= 2 else 0):
            ncols = P1_TMW[gi]
            cx.dma(bt_s[:, 0:ncols], btm[gi, :, 0:ncols], writes=[btB])

            def consume_tm(i, p, pB, ncols=ncols, coff=coff):
                v, vB = vs.next()
                cx.op("dve", lambda e: e.tensor_tensor(out=v[:, 0:ncols], in0=p[:, 0:ncols], in1=bt_s[:, 0:ncols], op=ALU.add),
                      reads=[pB, btB], writes=[vB])
                ob = Buf("o"); outs.append(ob)
                cx.dma(V[i * 128:(i + 1) * 128, coff:coff + ncols], v[:, 0:ncols], reads=[vB], writes=[ob],
                       queue="pool", owner=vB)

            tm.run(wtm[gi, :, :, 0:ncols], ncols, NQT, consume_tm)
            coff += ncols

        w1f, w1fB = env.sb([128, 32, 256], F32, "w1f")
        w1b, w1bB = env.sb([128, 32, 256], BF16, "w1b")
        w2f, w2fB = env.sb([128, 2, 64], F32, "w2f")
        w2b, w2bB = env.sb([128, 2, 64], BF16, "w2b")
        pos_s, posB = env.sb([128, 32], F32, "pos_s")
        tmp, tmpB = env.sb([128, 32, 128], BF16, "cmp_tmp")
        xs, xsB = env.sb([128, 512], F32, "cmp_xs")
        u1, u1B = env.sb([128, 512], F32, "cmp_u1")
        u2, u2B = env.sb([128, 512], F32, "cmp_u2")
        gl, glB = env.sb([128, 4, 128], BF16, "cmp_gl")
        kc_s, kcsB = env.sb([64, 2, 128], BF16, "kc_s")
        vc_s, vcsB = env.sb([128, 128], BF16, "vc_s")
        for kind in range(2 if STAGE >= 3 else 0):
            for half in range(2):
                cx.dma(w1f[64 * half:64 * half + 64], w1[kind], writes=[w1fB])
            cx.dma(w2f[:], w2[kind], writes=[w2fB])
            cx.dma(pos_s[:], pos[kind], writes=[posB])
            for q4 in range(4):
                eng = "pool" if q4 % 2 == 0 else "dve"
                cx.op(eng, lambda e: e.tensor_copy(out=w1b[:, q4 * 8:(q4 + 1) * 8, :], in_=w1f[:, q4 * 8:(q4 + 1) * 8, :]),
                      reads=[w1fB], writes=[w1bB])
            cx.op("dve", lambda e: e.tensor_copy(out=w2b[:], in_=w2f[:]), reads=[w2fB], writes=[w2bB])
            for p_ in range(32):
                cx.op("dve", lambda e: e.tensor_scalar(out=tmp[:, p_, :], in0=bc[:, kind, p_:p_ + 16 * 127 + 1:16],
                                                       scalar1=pos_s[:, p_:p_ + 1], scalar2=None, op0=ALU.add),
                      reads=[bcB, posB], writes=[tmpB])
            SUB = 9
            if SUB < 1:
                continue
            for g in range(2):
                for m in range(2):
                    sl = m * 128
                    pc, pcB = pcs[g]
                    for p_ in range(32):
                        cx.op("pe", lambda e: e.matmul(pc[:, sl:sl + 128], lhsT=w1b[64 * g:64 * g + 64, p_, m * 128:(m + 1) * 128],
                                                       rhs=tmp[64 * g:64 * g + 64, p_, :], start=(p_ == 0), stop=(p_ == 31)),
                              reads=[w1bB, tmpB], writes=[pcB])
            if SUB < 2:
                continue
            for g in range(2):
                pc, pcB = pcs[g]
                cx.op("act", lambda e: e.activation(out=xs[:, g * 256:(g + 1) * 256], in_=pc[:, 0:256], func=AF.Identity, scale=1.0),
                      reads=[pcB], writes=[xsB])
            cx.op("dve", lambda e: e.tensor_tensor(out=u1[:], in0=xs[:], in1=xs[:], op=ALU.mult), reads=[xsB], writes=[u1B])
            cx.op("dve", lambda e: e.tensor_scalar(out=u1[:], in0=u1[:], scalar1=0.044715, scalar2=1.0, op0=ALU.mult, op1=ALU.add),
                  reads=[u1B], writes=[u1B])
            cx.op("dve", lambda e: e.tensor_tensor(out=u2[:], in0=u1[:], in1=xs[:], op=ALU.mult), reads=[u1B, xsB], writes=[u2B])
            cx.op("act", lambda e: e.activation(out=u1[:], in_=u2[:], func=AF.Sigmoid, scale=1.5957691216057308),
                  reads=[u2B], writes=[u1B])
            cx.op("dve", lambda e: e.tensor_tensor(out=gl[:].rearrange("p a b -> p (a b)"), in0=xs[:], in1=u1[:], op=ALU.mult),
                  reads=[xsB, u1B], writes=[glB])
            if SUB < 3:
                continue
            for g in range(2):
                if kind == 0:
                    for m in range(2):
                        cx.op("pe", lambda e: e.matmul(pd[0:64, g * 128:(g + 1) * 128], lhsT=w2b[:, m, :], rhs=gl[:, g * 2 + m, :],
                                                       start=(m == 0), stop=(m == 1)), reads=[w2bB, glB], writes=[pdB])
                else:
                    for m in range(2):
                        cx.op("pe", lambda e: e.matmul(pd[:, 256 + g * 64:256 + (g + 1) * 64], lhsT=gl[:, g * 2 + m, :], rhs=w2b[:, m, :],
                                                       start=(m == 0), stop=(m == 1)), reads=[w2bB, glB], writes=[pdB])
            if kind == 0:
                cx.op("dve", lambda e: e.tensor_copy(out=kc_s[:].rearrange("p a b -> p (a b)"), in_=pd[0:64, 0:256]),
                      reads=[pdB], writes=[kcsB])
                for g in range(2):
                    ob = Buf("o"); outs.append(ob)
                    cx.dma(kcT[64 * g:64 * g + 64, :], kc_s[:, g, :], reads=[kcsB], writes=[ob], queue="pool", owner=kcsB)
            else:
                cx.op("dve", lambda e: e.tensor_copy(out=vc_s[:], in_=pd[:, 256:384]), reads=[pdB], writes=[vcsB])
                ob = Buf("o"); outs.append(ob)
                cx.dma(vc, vc_s[:], reads=[vcsB], writes=[ob], queue="pool", owner=vcsB)
        cx.wait_all("pool", outs)
        cx.wait_all("sp", outs)
    return nc


def _fm_tile(w, cols):
    return np.ascontiguousarray(w[:, cols].reshape(8, 128, len(cols)).transpose(1, 0, 2))


def p1_weights(w_in_l, b_in_l, w_cmp1_l, w_cmp2_l, cmp_pos_l):
    ar = np.arange
    fm_cols = [O_AK + 128 * c + ar(128) for c in range(12)] + [O_BSK + ar(128), O_BWK + ar(128), O_CK + ar(128),
                                                               O_BCK + ar(128), O_BCV + ar(128)]
    wfm = np.stack([_fm_tile(w_in_l, c) for c in fm_cols])
    bfm = np.ascontiguousarray(np.stack([b_in_l[c] for c in fm_cols], axis=1))
    tm_cols = [O_AV + 512 * g + ar(512) for g in range(3)] + [np.concatenate([O_BSV + ar(128), O_BWV + ar(128), O_CV + ar(128)])]
    wtm = np.zeros((P1_NTM, 128, 8, 512), np.float32)
    btm = np.zeros((P1_NTM, 128, 512), np.float32)
    for i, c in enumerate(tm_cols):
        wtm[i, :, :, :len(c)] = _fm_tile(w_in_l, c)
        btm[i, :, :len(c)] = b_in_l[c][None, :]
    w1 = np.ascontiguousarray(w_cmp1_l.reshape(2, 32, 64, 256).transpose(0, 2, 1, 3))
    w2 = np.ascontiguousarray(w_cmp2_l.reshape(2, 2, 128, 64).transpose(0, 2, 1, 3))
    posT = cmp_pos_l.transpose(0, 2, 1)
    pos = np.ascontiguousarray(np.concatenate([posT, posT], axis=1))
    return dict(wfm=wfm, bfm=bfm, wtm=wtm, btm=btm, w1=w1, w2=w2, pos=pos)


def xT_for_core(x_b, j, halo):
    n = T + halo
    seg = np.zeros((n, D), np.float32)
    hi = min(SEQ, j * T + n)
    seg[:hi - j * T] = x_b[j * T:hi]
    return np.ascontiguousarray(seg.T.reshape(8, 128, n))


_NC_CACHE = {}
RUN_KW = {}


def get_nc(name, builder):
    if name not in _NC_CACHE:
        _NC_CACHE[name] = builder()
    return _NC_CACHE[name]


def run_p1(x_full, wts):
    nc = get_nc("p1", build_p1)
    in_maps = []
    for c in range(NCORE):
        b, j = divmod(c, 4)
        m = dict(wts)
        m["xT"] = xT_for_core(x_full[b], j, 16)
        in_maps.append(m)
    res = run_bass_kernel_spmd(nc, in_maps, core_ids=list(range(NCORE)))
    return res.results


A_NT = (17, 20, 32)
P2_NFM_A = 16
SEL_T0 = 48


def a_qtiles(g):
    r = A_DIL[g]
    nu = 16 // r
    return r, nu, [(c, u) for c in range(r) for u in range(nu)]


class St:
    pass


def build_p2(stop_after=None):
    nc = bass.Bass("TRN2", target_bir_lowering=False)
    es = ExitStack()
    with es:
        st = St()
        st.nc, st.es = nc, es
        st.cx = cx = Ctx(nc, es)
        st.env = env = Env(nc, es, cx)
        din = env.din
        st.xT = din("xT", [8, 128, T], F32)
        identf_d = din("ident", [128, 128], F32)
        st.wA = din("wA", [P2_NFM_A, 128, 8, 128], F32)
        st.bA = din("bA", [128, P2_NFM_A], F32)
        st.KA = [din("KA%d" % g, [4, 128, A_NT[g] * 128], BF16) for g in range(3)]
        st.VA = [din("VA%d" % g, [A_NT[g], 128, 512], BF16) for g in range(3)]
        st.TBA = din("TBA", [4, 128, 3 * 2 * 4 * 128], F32)
        st.wQ = din("wQ", [16, 128, 8, 64], F32)
        st.bQ = din("bQ", [64, 16], F32)
        st.wT = din("wT", [3, 128, 8, 512], F32)
        st.bT = din("bT", [3, 128, 512], F32)
        st.QAUG = din("QAUG", [4, 2 * 16 * 4 * 128], BF16)
        st.Ksel = din("Ksel", [2, 68, 8192], BF16)
        st.Vsel = din("Vsel", [64, 128, 2, 65], BF16)
        st.Kwin = din("Kwin", [2, 68, 2560], BF16)
        st.Vwin = din("Vwin", [20, 128, 2, 65], BF16)
        st.Kcmp = din("Kcmp", [2, 68, 512], BF16)
        st.Vcmp = din("Vcmp", [4, 128, 2, 193], BF16)
        st.Kc = din("Kc", [2, 68, 2176], BF16)
        st.Vc = din("Vc", [17, 128, 2, 65], BF16)
        st.Mcmp = din("Mcmp", [128, 17, 128], BF16)
        st.CAUS = din("CAUS", [128, 2, 128], BF16)
        st.ALF = din("ALF", [16, 128, 2, 128], F32)
        st.EXP = din("EXP", [128, 8192], BF16)
        st.sinks = din("sinks", [128, 8], F32)
        st.wM = din("wM", [24, 128, 8, 128], F32)
        st.bM = din("bM", [128, 24], F32)
        st.wBr = din("wBr", [3, 128, 4, 1024], F32)
        st.wO = din("wO", [128, 8, 1024], F32)
        st.lng = din("lng", [128, 8], F32)
        st.lnb = din("lnb", [128, 8], F32)
        st.xnT, _ = env.dout("xnT", [8, 128, T], F32)
        st.yT, _ = env.dout("yT", [3, 4, 128, T], BF16)
        st.outs = []
        st.yB = [[Buf("yT%d_%d" % (b_, c_)) for c_ in range(4)] for b_ in range(3)]
        st.banks = [env.ps("ps%d" % i) for i in range(8)]
        ident_f, identfB = env.sb([128, 128], F32, "ident_f")
        st.ident, st.identB = env.sb([128, 128], BF16, "ident")
        st.ones_b, st.onesB = env.sb([128, 128], BF16, "ones_b")
        st.ones_f, st.onesfB = env.sb([128, 128], F32, "ones_f")
        cx.dma(ident_f[:], identf_d, writes=[identfB])
        cx.op("dve", lambda e: e.tensor_copy(out=st.ident[:], in_=ident_f[:]), reads=[identfB], writes=[st.identB])
        cx.op("pool", lambda e: e.memset(st.ones_b[:], 1.0), writes=[st.onesB])
        cx.op("pool", lambda e: e.memset(st.ones_f[:], 1.0), writes=[st.onesfB])
        phases = [("A", lambda: phase_A(st)), ("B", lambda: phase_gqa(st, True)), ("C", lambda: phase_gqa(st, False)),
                  ("F", lambda: phase_final(st))]
        for name, fn in phases:
            fn()
            cx.barrier()
            if stop_after == name:
                break
        allo = st.outs + [b for row in st.yB for b in row]
        cx.wait_all("pool", allo)
        cx.wait_all("sp", allo)
        st.es = None
    return nc


def phase_A(st):
    cx, env, banks = st.cx, st.env, st.banks
    with ExitStack() as ph:
        env.es = ph
        xb, xbB = env.sb([128, 8, T], BF16, "xbA")
        load_xT(env, st.xT, T, xb, xbB)
        bA_s, bAB = env.sb([128, P2_NFM_A], F32, "bA_s")
        cx.dma(bA_s[:], st.bA, writes=[bAB])
        fm = FMProj(env, xb, xbB, banks[0:2])
        qA = [env.sb([128, T], BF16, "qA%d" % g) for g in range(3)]
        agT, agB = env.sb([128, T], BF16, "agT")
        ka = [env.sb([128, A_NT[g] * 128], BF16, "ka%d" % g) for g in range(3)]
        va = [env.sb([128, A_NT[g], 128], BF16, "va%d" % g) for g in range(3)]
        tb, tbB = env.sb([128, 3, 2, 4, 128], F32, "tbA")
        Oacc, OaccB = env.sb([128, T], F32, "Oacc")
        Dacc, DaccB = env.sb([128, T], F32, "Dacc")
        yaT = Rot([env.sb([128, T], BF16) for _ in range(2)])
        Et = Rot([env.sb([128, 4, 128], BF16) for _ in range(4)])
        Pt = Rot([env.sb([128, 4, 128], BF16) for _ in range(4)])
        Sbk = [Rot(banks[2:4]), Rot(banks[4:6])]
        ODbk = Rot(banks[6:8])
        ones_b, onesB = st.ones_b, st.onesB
        tiles4 = [(i * 512, 512) for i in range(4)]
        for pr in range(4):
            def consume_a(tag, ti, t0, nt, p, pB):
                if tag < 12:
                    g = tag // 4
                    r = A_DIL[g]
                    q, qB_ = qA[g]
                    m0 = t0 // r
                    cx.op("act", lambda e: e.activation(
                        out=q[:].rearrange("p (c m) -> p m c", c=r)[:, m0:m0 + nt // r, :],
                        in_=p[:, 0:nt].rearrange("p (m c) -> p m c", c=r),
                        func=AF.Identity, bias=bA_s[:, tag:tag + 1], scale=1.0), reads=[pB, bAB], writes=[qB_])
                else:
                    cx.op("act", lambda e: e.activation(out=agT[:, t0:t0 + nt], in_=p[:, 0:nt], func=AF.Silu,
                                                        bias=bA_s[:, tag:tag + 1], scale=1.0), reads=[pB, bAB], writes=[agB])
            chunks = [(st.wA[g * 4 + pr], 128, g * 4 + pr) for g in range(3)] + [(st.wA[12 + pr], 128, 12 + pr)]
            fm.run(chunks, tiles4, consume_a)
            for g in range(3):
                cx.dma(ka[g][0][:], st.KA[g][pr], writes=[ka[g][1]])
                cx.dma(va[g][0][:], st.VA[g].rearrange("t k c -> k t c")[:, :, pr * 128:(pr + 1) * 128], writes=[va[g][1]])
            cx.dma(tb[:].rearrange("p a b c d -> p (a b c d)"), st.TBA[pr], writes=[tbB])
            cx.op("pool", lambda e: e.memset(Oacc[:], 0.0), writes=[OaccB])
            cx.op("pool", lambda e: e.memset(Dacc[:], 0.0), writes=[DaccB])
            units = [(g, hh, un) for g in range(3) for hh in range(2) for un in range(8)]

            def a_stage1(unit):
                g, hh, un = unit
                r, nu, qts = a_qtiles(g)
                q, qB_ = qA[g]
                k_, kB_ = ka[g]
                ps_ = slice(64 * hh, 64 * hh + 64)
                S, SB = Sbk[hh].next()
                E, EB = Et.next()
                P, PB = Pt.next()
                info = []
                for a in range(2):
                    c, u = qts[2 * un + a]
                    qi = c * nu + u
                    kprev = c * (1 + nu) + u
                    info.append((c, u, kprev, kprev + 1))
                    for b_ in range(2):
                        kt = kprev + b_
                        cx.op("pe", lambda e: e.matmul(S[:, (2 * a + b_) * 128:(2 * a + b_ + 1) * 128],
                                                       lhsT=k_[ps_, kt * 128:(kt + 1) * 128],
                                                       rhs=q[ps_, qi * 128:(qi + 1) * 128], start=True, stop=True),
                              reads=[kB_, qB_], writes=[SB])
                cx.op("act", lambda e: e.activation(out=E[:].rearrange("p a b -> p (a b)"), in_=S[:], func=AF.Exp, scale=0.125),
                      reads=[SB], writes=[EB])
                for a in range(2):
                    c, u, kp, kd = info[a]
                    ty = 0 if u == 0 else 2
                    cx.op("dve", lambda e: e.tensor_tensor(out=P[:, 2 * a:2 * a + 2, :], in0=E[:, 2 * a:2 * a + 2, :],
                                                           in1=tb[:, g, hh, ty:ty + 2, :], op=ALU.mult),
                          reads=[EB, tbB], writes=[PB])
                return (P, PB, info)

            def a_stage2(unit, h):
                g, hh, un = unit
                P, PB, info = h
                r = A_DIL[g]
                v_, vB_ = va[g]
                ps_ = slice(64 * hh, 64 * hh + 64)
                OD, ODB = ODbk.next()
                for a in range(2):
                    c, u, kp, kd = info[a]
                    for b_, kt in enumerate((kp, kd)):
                        cx.op("pe", lambda e: e.matmul(OD[:, a * 128:(a + 1) * 128], lhsT=v_[:, kt, :], rhs=P[:, 2 * a + b_, :],
                                                       start=(b_ == 0), stop=(b_ == 1)), reads=[vB_, PB], writes=[ODB])
                    for b_ in range(2):
                        cx.op("pe", lambda e: e.matmul(OD[:, 256 + a * 128:256 + (a + 1) * 128], lhsT=ones_b[:], rhs=P[:, 2 * a + b_, :],
                                                       start=(b_ == 0), stop=(b_ == 1)), reads=[onesB, PB], writes=[ODB])
                for a in range(2):
                    c, u, kp, kd = info[a]
                    t0 = c + 128 * u * r
                    tsl = slice(t0, t0 + 127 * r + 1, r)
                    cx.op("dve", lambda e: e.tensor_tensor(out=Oacc[ps_, tsl], in0=OD[ps_, a * 128:(a + 1) * 128],
                                                           in1=Oacc[ps_, tsl], op=ALU.add), reads=[ODB, OaccB], writes=[OaccB])
                    cx.op("dve", lambda e: e.tensor_tensor(out=Dacc[ps_, tsl], in0=OD[ps_, 256 + a * 128:256 + (a + 1) * 128],
                                                           in1=Dacc[ps_, tsl], op=ALU.add), reads=[ODB, DaccB], writes=[DaccB])

            hs = {0: a_stage1(units[0]), 1: a_stage1(units[1])}
            for ui in range(len(units)):
                if ui + 2 < len(units):
                    hs[ui + 2] = a_stage1(units[ui + 2])
                a_stage2(units[ui], hs.pop(ui))
            ya, yaB = yaT.next()
            cx.op("dve", lambda e: e.reciprocal(out=Dacc[:], in_=Dacc[:]), reads=[DaccB], writes=[DaccB])
            cx.op("pool", lambda e: e.tensor_tensor(out=Oacc[:], in0=Oacc[:], in1=Dacc[:], op=ALU.mult),
                  reads=[OaccB, DaccB], writes=[OaccB])
            cx.op("pool", lambda e: e.tensor_tensor(out=ya[:], in0=Oacc[:], in1=agT[:], op=ALU.mult),
                  reads=[OaccB, agB], writes=[yaB])
            cx.dma(st.yT[0, pr], ya[:], reads=[yaB], writes=[st.yB[0][pr]], queue="pool", owner=yaB)
        cx.barrier()
        cx.retire([b for _, b in fm.wf.items] + [b for _, b in ka] + [b for _, b in va] + [tbB, bAB] + [b for _, b in yaT.items])
        env.es = st.es


def phase_gqa(st, isB):
    cx, env, banks = st.cx, st.env, st.banks
    hoff = 0 if isB else 8
    ybr = 1 if isB else 2
    with ExitStack() as ph:
        env.es = ph
        Qaug, QB = env.sb([68, 2 * 16 * 4 * 128], BF16, "Qaug")
        gS, gSB = env.sb([128, 16, 512], BF16, "gateS")
        gsig, gsigB = env.sb([128, 16, 24], F32, "gsig")
        with ExitStack() as ph2:
            env.es = ph2
            xb, xbB = env.sb([128, 8, T], BF16, "xbG")
            load_xT(env, st.xT, T, xb, xbB)
            bQ_s, bQB = env.sb([64, 16], F32, "bQ_s")
            cx.dma(bQ_s[:], st.bQ, writes=[bQB])
            fm = FMProj(env, xb, xbB, banks[2:4], mmax=64)
            Qv = Qaug[:].rearrange("p (g i h q) -> p g i h q", g=2, i=16, h=4)

            def consume_q(tag, ti, t0, nt, p, pB):
                g, hh = divmod(tag, 4)
                cx.op("act", lambda e: e.activation(out=Qv[0:64, g, 4 * ti:4 * ti + 4, hh, :],
                                                    in_=p[0:64, 0:512].rearrange("p (i q) -> p i q", i=4),
                                                    func=AF.Identity, bias=bQ_s[:, hoff + tag:hoff + tag + 1], scale=1.0),
                      reads=[pB, bQB], writes=[QB])
            fm.run([(st.wQ[hoff + h], 64, h) for h in range(8)], [(i * 512, 512) for i in range(4)], consume_q)
            tm = TMProj(env, xb, xbB, banks[4:6])
            bt_s, btB = env.sb([128, 512], F32, "bt_s")
            tmpf = Rot([env.sb([128, 512], F32) for _ in range(2)])
            for (gi, ncols, func, dst) in ([(0, 512, AF.Silu, gS), (1, 24, AF.Sigmoid, gsig)] if isB else [(2, 512, AF.Silu, gS)]):
                cx.dma(bt_s[:, 0:ncols], st.bT[gi, :, 0:ncols], writes=[btB])
                dB = gSB if dst is gS else gsigB

                def consume_t(i, p, pB, ncols=ncols, func=func, dst=dst, dB=dB):
                    t_, tB_ = tmpf.next()
                    cx.op("dve", lambda e: e.tensor_tensor(out=t_[:, 0:ncols], in0=p[:, 0:ncols], in1=bt_s[:, 0:ncols], op=ALU.add),
                          reads=[pB, btB], writes=[tB_])
                    cx.op("act", lambda e: e.activation(out=dst[:, i, 0:ncols], in_=t_[:, 0:ncols], func=func), reads=[tB_], writes=[dB])
                tm.run(st.wT[gi, :, :, 0:ncols], ncols, NQT, consume_t)
            cx.barrier()
            cx.retire([b for _, b in fm.wf.items] + [tm.wfB, btB, bQB])
            env.es = ph
        cx.dma(Qaug[64:68, :], st.QAUG, writes=[QB])
        caus, causB = env.sb([128, 2, 128], BF16, "caus")
        cx.dma(caus[:], st.CAUS, writes=[causB])
        Sb = Rot(banks[0:2])
        Et = Rot([env.sb([128, 4, 128], BF16) for _ in range(4)])
        Pt = Rot([env.sb([128, 4, 128], BF16) for _ in range(4)])
        ybacc, ybaccB = env.sb([128, 4, 64], F32, "ybacc")
        ytmp, ytmpB = env.sb([128, 4, 64], F32, "ytmp")
        ybf, ybfB = env.sb([128, 256], BF16, "ybf")
        ybT = Rot([env.sb([128, 2, T], BF16) for _ in range(2)])
        dn, dnB = env.sb([128, 4], F32, "dn")
        fac, facB = env.sb([128, 4], F32, "fac")
        TR, TRB = banks[3]
        TRb = TR[:].bitcast(BF16)
        retire = [causB, QB]
        if isB:
            mcmp, mcmpB = env.sb([128, 17, 128], BF16, "mcmp")
            cx.dma(mcmp[:], st.Mcmp, writes=[mcmpB])
            expn, expnB = env.sb([128, 8192], BF16, "expn")
            cx.dma(expn[:], st.EXP, writes=[expnB])
            ksel, kselB = env.sb([68, 8192], BF16, "ksel")
            vsel, vselB = env.sb([128, 64, 65], BF16, "vsel")
            kwin, kwinB = env.sb([68, 2560], BF16, "kwin")
            vwin, vwinB = env.sb([128, 20, 65], BF16, "vwin")
            kcmp, kcmpB = env.sb([68, 512], BF16, "kcmp")
            vcmp, vcmpB = env.sb([128, 4, 193], BF16, "vcmp")
            alf = Rot([env.sb([128, 2, 128], F32) for _ in range(2)])
            impacc, impB = env.sb([128, 128], F32, "impacc")
            score, scoreB = env.sb([128, 128], F32, "score")
            m8, m8B = env.sb([128, 8], F32, "m8")
            wk1, wk1B = env.sb([128, 128], F32, "wk1")
            wk2, wk2B = env.sb([128, 128], F32, "wk2")
            mbf, mbfB = env.sb([128, 128], BF16, "mbf")
            selT, selTB = env.sb([128, 128], BF16, "selT")
            mdg, mdgB = env.sb([128, 128], BF16, "mdg")
            MKs = [(banks[2][0][:, 0:128], banks[2][1]), (banks[3][0][:, 256:384], banks[3][1])]
            retire += [mcmpB, expnB, kselB, vselB, kwinB, vwinB, kcmpB, vcmpB] + [b for _, b in alf.items]
        else:
            kc_, kcB_ = env.sb([68, 2176], BF16, "kc_")
            vc_, vcB_ = env.sb([128, 17, 65], BF16, "vc_")
            esk, eskB = env.sb([128, 8], F32, "esk")
            cx.dma(esk[:], st.sinks, writes=[eskB])
            cx.op("act", lambda e: e.activation(out=esk[:], in_=esk[:], func=AF.Exp), reads=[eskB], writes=[eskB])
            retire += [kcB_, vcB_, eskB]
        gsv = gsig[:].rearrange("p i (h c) -> p i h c", c=3)

        ODbk = banks[4:8]
        ODs, ODsB = env.sb([128, 4, 193], F32, "ODs")
        mb, mbB = env.sb([128, 4, 128], BF16, "mbias")

        def tile_spec(rhsQ, kap, kB, vap, vB, W, first, last, mask=None, maskB=None, addmask=False, pre=None, post=None):
            return dict(rhsQ=rhsQ, kap=kap, kB=kB, vap=vap, vB=vB, W=W, first=first, last=last, mask=mask, maskB=maskB,
                        addmask=addmask, pre=pre, post=post)

        def stage1(sp):
            mask, maskB = sp["mask"], sp["maskB"]
            kap, kB, rhsQ = sp["kap"], sp["kB"], sp["rhsQ"]
            S, SB = Sb.next()
            P, PB = Pt.next()
            if sp["addmask"]:
                cx.op("dve", lambda e: e.tensor_scalar(out=mb[:], in0=mask.unsqueeze(1).to_broadcast([128, 4, 128]), scalar1=-1.0,
                                                       scalar2=65536.0, op0=ALU.add, op1=ALU.mult), reads=[maskB], writes=[mbB])
                cx.op("pe", lambda e: e.matmul(S[:, 0:512], lhsT=kap, rhs=rhsQ, start=True, stop=False), reads=[kB, QB], writes=[SB])
                cx.op("pe", lambda e: e.matmul(S[:, 0:512], lhsT=st.ident[:], rhs=mb[:].rearrange("p a b -> p (a b)"), start=False, stop=True),
                      reads=[st.identB, mbB], writes=[SB])
                mask = None
            else:
                cx.op("pe", lambda e: e.matmul(S[:, 0:512], lhsT=kap, rhs=rhsQ, start=True, stop=True), reads=[kB, QB], writes=[SB])
            if sp["pre"] is not None:
                mask, maskB = sp["pre"]()
            if mask is None:
                cx.op("act", lambda e: e.activation(out=P[:].rearrange("p a b -> p (a b)"), in_=S[:, 0:512], func=AF.Exp, scale=0.125),
                      reads=[SB], writes=[PB])
            else:
                E, EB = Et.next()
                cx.op("act", lambda e: e.activation(out=E[:].rearrange("p a b -> p (a b)"), in_=S[:, 0:512], func=AF.Exp, scale=0.125),
                      reads=[SB], writes=[EB])
                cx.op("dve", lambda e: e.tensor_tensor(out=P[:], in0=E[:], in1=mask.unsqueeze(1).to_broadcast([128, 4, 128]), op=ALU.mult),
                      reads=[EB, maskB], writes=[PB])
            return (P, PB)

        def stage2(sp, h):
            P, PB = h
            W = sp["W"]
            for hh in range(4):
                O_, OB_ = ODbk[hh]
                cx.op("pe", lambda e: e.matmul(O_[:, 0:W], lhsT=P[:, hh, :], rhs=sp["vap"], start=sp["first"], stop=sp["last"]),
                      reads=[PB, sp["vB"]], writes=[OB_])
            if sp["last"]:
                for hh in range(4):
                    O_, OB_ = ODbk[hh]
                    cx.op("act", lambda e: e.activation(out=ODs[:, hh, 0:W], in_=O_[:, 0:W], func=AF.Identity), reads=[OB_], writes=[ODsB])
                if sp["post"] is not None:
                    sp["post"]()

        def run_tiles(specs, depth=2):
            hs = {}
            n = len(specs)
            for t in range(min(depth, n)):
                hs[t] = stage1(specs[t])
            for t in range(n):
                if t + depth < n:
                    hs[t + depth] = stage1(specs[t + depth])
                stage2(specs[t], hs.pop(t))

        for g in range(2):
            if isB:
                cx.dma(ksel[:], st.Ksel[g], writes=[kselB])
                cx.dma(vsel[:], st.Vsel.rearrange("t k g w -> k t g w")[:, :, g, :], writes=[vselB])
                cx.dma(kwin[:], st.Kwin[g], writes=[kwinB])
                cx.dma(vwin[:], st.Vwin.rearrange("t k g w -> k t g w")[:, :, g, :], writes=[vwinB])
                cx.dma(kcmp[:], st.Kcmp[g], writes=[kcmpB])
                cx.dma(vcmp[:], st.Vcmp.rearrange("t k g w -> k t g w")[:, :, g, :], writes=[vcmpB])
            else:
                cx.dma(kc_[:], st.Kc[g], writes=[kcB_])
                cx.dma(vc_[:], st.Vc.rearrange("t k g w -> k t g w")[:, :, g, :], writes=[vcB_])
            yt_, ytB_ = ybT.next()
            for i in range(NQT):
                qo = ((g * 16 + i) * 4) * 128
                rhsQ = Qaug[:, qo:qo + 512]
                if isB:
                    specs = []
                    for c_ in range(4):
                        mk = None
                        if c_ == 3:
                            mk = mcmp[:, i, :]
                        elif c_ == 2 and i == 0:
                            mk = mcmp[:, 16, :]
                        specs.append(tile_spec(rhsQ, kcmp[:, c_ * 128:(c_ + 1) * 128], kcmpB, vcmp[:, c_, :], vcmpB, 193,
                                               c_ == 0, c_ == 3, mk, mcmpB, addmask=(mk is not None)))
                    run_tiles(specs)
                    a_, aB_ = alf.next()
                    cx.dma(a_[:], st.ALF[i], writes=[aB_])
                    cx.op("dve", lambda e: e.tensor_scalar_max(out=dn[:], in0=ODs[:, :, 64], scalar1=1e-30), reads=[ODsB], writes=[dnB])
                    cx.op("dve", lambda e: e.reciprocal(out=dn[:], in_=dn[:]), reads=[dnB], writes=[dnB])
                    cx.op("dve", lambda e: e.tensor_tensor(out=fac[:], in0=dn[:], in1=gsv[:, i, 4 * g:4 * g + 4, 0], op=ALU.mult),
                          reads=[dnB, gsigB], writes=[facB])
                    cx.op("dve", lambda e: e.tensor_tensor(out=ybacc[:], in0=ODs[:, :, 0:64], in1=fac[:].unsqueeze(2).to_broadcast([128, 4, 64]),
                                                           op=ALU.mult), reads=[ODsB, facB], writes=[ybaccB])
                    cx.op("dve", lambda e: e.tensor_scalar(out=impacc[:], in0=ODs[:, 0, 65:193], scalar1=dn[:, 0:1], scalar2=None, op0=ALU.mult),
                          reads=[ODsB, dnB], writes=[impB])
                    for hh in range(1, 4):
                        cx.op("dve", lambda e: e.scalar_tensor_tensor(out=impacc[:], in0=ODs[:, hh, 65:193], scalar=dn[:, hh:hh + 1], in1=impacc[:],
                                                                      op0=ALU.mult, op1=ALU.add), reads=[ODsB, dnB, impB], writes=[impB])
                    cx.op("dve", lambda e: e.tensor_tensor(out=score[:], in0=impacc[:], in1=a_[:, 0, :], op=ALU.mult), reads=[impB, aB_], writes=[scoreB])
                    cx.op("dve", lambda e: e.tensor_tensor(out=score[:], in0=score[:], in1=a_[:, 1, :], op=ALU.add), reads=[scoreB, aB_], writes=[scoreB])
                    cx.op("dve", lambda e: e.max(out=m8[:], in_=score[:]), reads=[scoreB], writes=[m8B])
                    cx.op("dve", lambda e: e.match_replace(out=wk1[:], in_to_replace=m8[:], in_values=score[:], imm_value=-3.0),
                          reads=[scoreB, m8B], writes=[wk1B])
                    cx.op("dve", lambda e: e.max(out=m8[:], in_=wk1[:]), reads=[wk1B], writes=[m8B])
                    cx.op("dve", lambda e: e.match_replace(out=wk2[:], in_to_replace=m8[:], in_values=wk1[:], imm_value=-3.0),
                          reads=[wk1B, m8B], writes=[wk2B])
                    cx.op("dve", lambda e: e.tensor_tensor(out=wk1[:], in0=score[:], in1=wk2[:], op=ALU.subtract), reads=[scoreB, wk2B], writes=[wk1B])
                    cx.op("dve", lambda e: e.scalar_tensor_tensor(out=mbf[:], in0=wk1[:], scalar=1.0, in1=a_[:, 0, :], op0=ALU.min, op1=ALU.mult),
                          reads=[wk1B, aB_], writes=[mbfB])
                    cx.op("pe", lambda e: e.transpose(TRb[:, 0:128], mbf[:], st.ident[:]), reads=[mbfB, st.identB], writes=[TRB])
                    cx.op("act", lambda e: e.activation(out=selT[:], in_=TRb[:, 0:128], func=AF.Identity), reads=[TRB], writes=[selTB])
                    def gate_epilogue(ci):
                        cx.op("dve", lambda e: e.tensor_scalar_max(out=dn[:], in0=ODs[:, :, 64], scalar1=1e-30), reads=[ODsB], writes=[dnB])
                        cx.op("dve", lambda e: e.reciprocal(out=dn[:], in_=dn[:]), reads=[dnB], writes=[dnB])
                        cx.op("dve", lambda e: e.tensor_tensor(out=fac[:], in0=dn[:], in1=gsv[:, i, 4 * g:4 * g + 4, ci], op=ALU.mult),
                              reads=[dnB, gsigB], writes=[facB])
                        cx.op("dve", lambda e: e.tensor_tensor(out=ytmp[:], in0=ODs[:, :, 0:64], in1=fac[:].unsqueeze(2).to_broadcast([128, 4, 64]),
                                                               op=ALU.mult), reads=[ODsB, facB], writes=[ytmpB])
                        cx.op("pool", lambda e: e.tensor_tensor(out=ybacc[:], in0=ybacc[:], in1=ytmp[:], op=ALU.add),
                              reads=[ybaccB, ytmpB], writes=[ybaccB])
                    nkt = SEL_T0 + i + 1
                    specs = []
                    for kt in range(nkt):
                        def pre(kt=kt, nkt=nkt):
                            mka, mkb = MKs[kt % 2]
                            cx.op("pe", lambda e: e.matmul(mka, lhsT=expn[:, kt * 128:(kt + 1) * 128], rhs=selT[:],
                                                           start=True, stop=True), reads=[expnB, selTB], writes=[mkb])
                            if kt == nkt - 1:
                                cx.op("dve", lambda e: e.tensor_tensor(out=mdg[:], in0=mka, in1=caus[:, 0, :], op=ALU.mult),
                                      reads=[mkb, causB], writes=[mdgB])
                                return mdg[:], mdgB
                            return mka, mkb
                        specs.append(tile_spec(rhsQ, ksel[:, kt * 128:(kt + 1) * 128], kselB, vsel[:, kt, :], vselB, 65, kt == 0, kt == nkt - 1,
                                               pre=pre, post=(lambda: gate_epilogue(1)) if kt == nkt - 1 else None))
                    for d in range(4, -1, -1):
                        wt = 4 + i - d
                        mk = caus[:, 1, :] if d == 4 else (caus[:, 0, :] if d == 0 else None)
                        specs.append(tile_spec(rhsQ, kwin[:, wt * 128:(wt + 1) * 128], kwinB, vwin[:, wt, :], vwinB, 65, d == 4, d == 0, mk, causB,
                                               post=(lambda: gate_epilogue(2)) if d == 0 else None))
                    run_tiles(specs)
                else:
                    specs = []
                    for d in (1, 0):
                        wt = 1 + i - d
                        mk = caus[:, 1, :] if d == 1 else caus[:, 0, :]
                        specs.append(tile_spec(rhsQ, kc_[:, wt * 128:(wt + 1) * 128], kcB_, vc_[:, wt, :], vcB_, 65, d == 1, d == 0, mk, causB))
                    run_tiles(specs)
                    cx.op("dve", lambda e: e.tensor_tensor(out=dn[:], in0=ODs[:, :, 64], in1=esk[:, 4 * g:4 * g + 4], op=ALU.add),
                          reads=[ODsB, eskB], writes=[dnB])
                    cx.op("dve", lambda e: e.reciprocal(out=dn[:], in_=dn[:]), reads=[dnB], writes=[dnB])
                    cx.op("dve", lambda e: e.tensor_tensor(out=ybacc[:], in0=ODs[:, :, 0:64], in1=dn[:].unsqueeze(2).to_broadcast([128, 4, 64]),
                                                           op=ALU.mult), reads=[ODsB, dnB], writes=[ybaccB])
                cx.op("pool", lambda e: e.tensor_tensor(out=ybf[:], in0=ybacc[:].rearrange("p h d -> p (h d)"),
                                                        in1=gS[:, i, g * 256:(g + 1) * 256], op=ALU.mult), reads=[ybaccB, gSB], writes=[ybfB])
                for cc in range(2):
                    cx.op("pe", lambda e: e.transpose(TRb[:, 256 + cc * 128:256 + (cc + 1) * 128], ybf[:, cc * 128:(cc + 1) * 128], st.ident[:]),
                          reads=[ybfB, st.identB], writes=[TRB])
                cx.op("act", lambda e: e.activation(out=yt_[:, :, i * 128:(i + 1) * 128],
                                                    in_=TRb[:, 256:512].rearrange("p (c q) -> p c q", c=2), func=AF.Identity),
                      reads=[TRB], writes=[ytB_])
            for cc in range(2):
                cx.dma(st.yT[ybr, 2 * g + cc], yt_[:, cc, :], reads=[ytB_], writes=[st.yB[ybr][2 * g + cc]], queue="pool", owner=ytB_)
        cx.barrier()
        cx.retire(retire + [b for _, b in ybT.items])
        env.es = st.es


def phase_final(st):
    cx, env, banks = st.cx, st.env, st.banks
    with ExitStack() as ph:
        env.es = ph
        bM_s, bMB = env.sb([128, 24], F32, "bM_s")
        cx.dma(bM_s[:], st.bM, writes=[bMB])
        lg, lgB = env.sb([128, 8], F32, "lg")
        lb, lbB = env.sb([128, 8], F32, "lb")
        cx.dma(lg[:], st.lng, writes=[lgB])
        cx.dma(lb[:], st.lnb, writes=[lbB])
        stg, stgB = env.sb([128, 4, 1024], F32, "wstg")
        wbr, wbrB = env.sb([128, 3, 4, 1024], BF16, "wbr")
        wo, woB = env.sb([128, 8, 1024], BF16, "wo")
        for br in range(3):
            cx.dma(stg[:], st.wBr[br], writes=[stgB])
            for h_ in range(4):
                eng = "pool" if h_ % 2 == 0 else "dve"
                cx.op(eng, lambda e: e.tensor_copy(out=wbr[:, br, h_, :], in_=stg[:, h_, :]), reads=[stgB], writes=[wbrB])
        for half in range(2):
            cx.dma(stg[:], st.wO[:, half * 4:(half + 1) * 4, :], writes=[stgB])
            for h_ in range(4):
                eng = "pool" if h_ % 2 == 0 else "dve"
                cx.op(eng, lambda e: e.tensor_copy(out=wo[:, half * 4 + h_, :], in_=stg[:, h_, :]), reads=[stgB], writes=[woB])
        x32, x32B = env.sb([128, 8, 512], F32, "x32")
        xbt, xbtB = env.sb([128, 8, 512], BF16, "xbt")
        yt, ytB = env.sb([128, 3, 4, 512], BF16, "yt")
        sg, sgB = env.sb([128, 512], F32, "sg")
        mrg, mrgB = env.sb([128, 8, 512], F32, "mrg")
        mrb, mrbB = env.sb([128, 8, 512], BF16, "mrb")
        z, zB = env.sb([128, 8, 512], F32, "z")
        zsq, zsqB = env.sb([128, 8, 512], F32, "zsq")
        mean, meanB = env.sb([128, 512], F32, "mean")
        rstd, rstdB = env.sb([128, 512], F32, "rstd")
        tmp, tmpB = env.sb([128, 512], F32, "lntmp")
        fm = FMProj(env, xbt, xbtB, banks[0:2], cast_engs=("act", "act", "pool"))
        Zb = Rot(banks[2:4])
        Yb = Rot(banks[4:6])
        (S1, S1B), (S2, S2B) = banks[6], banks[7]
        yrd = [b for row in st.yB for b in row]
        for tt in range(4):
            tsl = slice(tt * 512, (tt + 1) * 512)
            cx.dma(x32[:], st.xT.rearrange("k p t -> p k t")[:, :, tsl], writes=[x32B])
            for kc in range(8):
                eng = "pool" if kc % 2 == 0 else "dve"
                cx.op(eng, lambda e: e.tensor_copy(out=xbt[:, kc, :], in_=x32[:, kc, :]), reads=[x32B], writes=[xbtB])
            cx.dma(yt[:], st.yT.rearrange("b c p t -> p b c t")[:, :, :, tsl], reads=yrd, writes=[ytB])

            def consume_m(tag, ti, t0, nt, p, pB):
                br, dc = divmod(tag, 8)
                cx.op("act", lambda e: e.activation(out=sg[:], in_=p[:, 0:512], func=AF.Sigmoid, bias=bM_s[:, tag:tag + 1], scale=1.0),
                      reads=[pB, bMB], writes=[sgB])
                Zp, ZpB = Zb.next()
                for cc in range(4):
                    cx.op("pe", lambda e: e.matmul(Zp[:, 0:512], lhsT=wbr[:, br, cc, dc * 128:(dc + 1) * 128], rhs=yt[:, br, cc, :],
                                                   start=(cc == 0), stop=(cc == 3)), reads=[wbrB, ytB], writes=[ZpB])
                if br == 0:
                    cx.op("dve", lambda e: e.tensor_tensor(out=mrg[:, dc, :], in0=Zp[:, 0:512], in1=sg[:], op=ALU.mult),
                          reads=[ZpB, sgB], writes=[mrgB])
                else:
                    cx.op("dve", lambda e: e.tensor_tensor(out=tmp[:], in0=Zp[:, 0:512], in1=sg[:], op=ALU.mult),
                          reads=[ZpB, sgB], writes=[tmpB])
                    o_ = mrb[:, dc, :] if br == 2 else mrg[:, dc, :]
                    cx.op("pool" if br == 1 else "dve", lambda e: e.tensor_tensor(out=o_, in0=mrg[:, dc, :], in1=tmp[:], op=ALU.add),
                          reads=[mrgB, tmpB], writes=[mrbB if br == 2 else mrgB])
            fm.run([(st.wM[c], 128, c) for c in range(24)], [(0, 512)], consume_m)
            for dc in range(8):
                Yp, YpB = Yb.next()
                for kc in range(8):
                    cx.op("pe", lambda e: e.matmul(Yp[:, 0:512], lhsT=wo[:, kc, dc * 128:(dc + 1) * 128], rhs=mrb[:, kc, :],
                                                   start=(kc == 0), stop=(kc == 7)), reads=[woB, mrbB], writes=[YpB])
                cx.op("dve", lambda e: e.scalar_tensor_tensor(out=z[:, dc, :], in0=x32[:, dc, :], scalar=float(ALPHA), in1=Yp[:, 0:512],
                                                              op0=ALU.mult, op1=ALU.add), reads=[x32B, YpB], writes=[zB])
                cx.op("pool", lambda e: e.tensor_tensor(out=zsq[:, dc, :], in0=z[:, dc, :], in1=z[:, dc, :], op=ALU.mult),
                      reads=[zB], writes=[zsqB])
            for dc in range(8):
                cx.op("pe", lambda e: e.matmul(S1[:, 0:512], lhsT=st.ones_f[:], rhs=z[:, dc, :], start=(dc == 0), stop=(dc == 7)),
                      reads=[st.onesfB, zB], writes=[S1B])
            for dc in range(8):
                cx.op("pe", lambda e: e.matmul(S2[:, 0:512], lhsT=st.ones_f[:], rhs=zsq[:, dc, :], start=(dc == 0), stop=(dc == 7)),
                      reads=[st.onesfB, zsqB], writes=[S2B])
            cx.op("act", lambda e: e.activation(out=mean[:], in_=S1[:, 0:512], func=AF.Identity, scale=1.0 / D), reads=[S1B], writes=[meanB])
            cx.op("dve", lambda e: e.tensor_tensor(out=tmp[:], in0=mean[:], in1=mean[:], op=ALU.mult), reads=[meanB], writes=[tmpB])
            cx.op("dve", lambda e: e.scalar_tensor_tensor(out=rstd[:], in0=S2[:, 0:512], scalar=1.0 / D, in1=tmp[:], op0=ALU.mult, op1=ALU.subtract),
                  reads=[S2B, tmpB], writes=[rstdB])
            cx.op("dve", lambda e: e.tensor_scalar(out=rstd[:], in0=rstd[:], scalar1=float(LN_EPS), scalar2=None, op0=ALU.add),
                  reads=[rstdB], writes=[rstdB])
            cx.op("act", lambda e: e.activation(out=rstd[:], in_=rstd[:], func=AF.Sqrt), reads=[rstdB], writes=[rstdB])
            cx.op("dve", lambda e: e.reciprocal(out=rstd[:], in_=rstd[:]), reads=[rstdB], writes=[rstdB])
            for dc in range(8):
                cx.op("dve", lambda e: e.tensor_tensor(out=zsq[:, dc, :], in0=z[:, dc, :], in1=mean[:], op=ALU.subtract),
                      reads=[zB, meanB], writes=[zsqB])
                cx.op("pool", lambda e: e.tensor_tensor(out=zsq[:, dc, :], in0=zsq[:, dc, :], in1=rstd[:], op=ALU.mult),
                      reads=[zsqB, rstdB], writes=[zsqB])
                cx.op("dve", lambda e: e.tensor_scalar(out=z[:, dc, :], in0=zsq[:, dc, :], scalar1=lg[:, dc:dc + 1], scalar2=lb[:, dc:dc + 1],
                                                       op0=ALU.mult, op1=ALU.add), reads=[zsqB, lgB, lbB], writes=[zB])
            ob = Buf("xn%d" % tt)
            st.outs.append(ob)
            cx.dma(st.xnT.rearrange("k p t -> p k t")[:, :, tsl], z[:], reads=[zB], writes=[ob], queue="pool", owner=zB)
        cx.barrier()
        env.es = st.es


def p2_weights(w_in_l, b_in_l, w_branch_l, w_out_l, ln_g_l, ln_b_l, sinks_l):
    ar = np.arange
    colsA = [O_AQ + 128 * c + ar(128) for c in range(12)] + [O_AG + 128 * c + ar(128) for c in range(4)]
    wA = np.stack([_fm_tile(w_in_l, c) for c in colsA])
    bA = np.ascontiguousarray(np.stack([b_in_l[c] for c in colsA], axis=1))
    colsQ = [O_BQ + 64 * h + ar(64) for h in range(8)] + [O_CQ + 64 * h + ar(64) for h in range(8)]
    wQ = np.stack([_fm_tile(w_in_l, c) for c in colsQ])
    bQ = np.ascontiguousarray(np.stack([b_in_l[c] for c in colsQ], axis=1))
    wT = np.zeros((3, 128, 8, 512), np.float32)
    bT = np.zeros((3, 128, 512), np.float32)
    for i, c in enumerate([O_BG + ar(512), O_BGATE + ar(24), O_CG + ar(512)]):
        wT[i, :, :, :len(c)] = _fm_tile(w_in_l, c)
        bT[i, :, :len(c)] = b_in_l[c][None, :]
    colsM = [O_MG + 128 * c + ar(128) for c in range(24)]
    wM = np.stack([_fm_tile(w_in_l, c) for c in colsM])
    bM = np.ascontiguousarray(np.stack([b_in_l[c] for c in colsM], axis=1))
    wBr = np.ascontiguousarray(w_branch_l.reshape(3, 4, 128, 1024).transpose(0, 2, 1, 3))
    wO = np.ascontiguousarray(w_out_l.reshape(8, 128, 1024).transpose(1, 0, 2))
    lng = np.ascontiguousarray(ln_g_l.reshape(8, 128).T)
    lnb = np.ascontiguousarray(ln_b_l.reshape(8, 128).T)
    sinks = np.ascontiguousarray(np.broadcast_to(sinks_l[None, :], (128, 8))).astype(np.float32)
    return dict(wA=wA, bA=bA, wQ=wQ, bQ=bQ, wT=wT, bT=bT, wM=wM, bM=bM, wBr=wBr, wO=wO, lng=lng, lnb=lnb, sinks=sinks)


def _aug_rows(pos):
    pos = np.asarray(pos, np.int64)
    return np.stack([(pos // 128) * 128, pos % 128, np.ones_like(pos), np.ones_like(pos)]).astype(np.float32)


_CONST_CACHE = {}


def p2_consts(j):
    if j in _CONST_CACHE:
        return _CONST_CACHE[j]
    kq = np.arange(128)
    k_, q_ = kq[:, None], kq[None, :]
    out = {}
    out["ident"] = np.eye(128, dtype=np.float32)
    slopes_a = 2.0 ** (-8.0 * np.arange(1, 25) / 24.0)
    TBA = np.zeros((4, 128, 3, 2, 4, 128), np.float32)
    for g in range(3):
        for pr in range(4):
            for hh in range(2):
                sl = slopes_a[g * 8 + pr * 2 + hh] * A_DIL[g]
                diag = np.where(k_ <= q_, np.exp(-sl * (q_ - k_)), 0.0)
                prev = np.where(k_ >= q_, np.exp(-sl * (q_ - k_ + 128)), 0.0)
                TBA[pr, :, g, hh, 0] = prev if j > 0 else 0.0
                TBA[pr, :, g, hh, 1] = diag
                TBA[pr, :, g, hh, 2] = prev
                TBA[pr, :, g, hh, 3] = diag
    out["TBA"] = TBA.reshape(4, 128, -1)
    QA = np.zeros((4, 2, 16, 4, 128), np.float32)
    for g in range(2):
        for hh in range(4):
            s8 = 8.0 * 2.0 ** (-(4 * g + hh + 1))
            for i in range(16):
                QA[0, g, i, hh] = s8
                QA[1, g, i, hh] = s8
                QA[2, g, i, hh] = -s8 * 128 * (SEL_T0 + i)
                QA[3, g, i, hh] = -s8 * kq
    out["QAUG"] = QA.reshape(4, -1).astype(NPBF)
    M = np.zeros((128, 17, 128), np.float32)
    for i in range(16):
        M[:, i, :] = (128 * i + q_ >= 16 * k_ + 31)
    M[:, 16, :] = (q_ >= 16 * (k_ - 128) + 31)
    out["Mcmp"] = M.astype(NPBF)
    C = np.zeros((128, 2, 128), np.float32)
    C[:, 0, :] = (k_ <= q_)
    C[:, 1, :] = (k_ >= q_ + 1)
    out["CAUS"] = C.astype(NPBF)
    out["EXP"] = (np.arange(128)[:, None] == (np.arange(8192)[None, :] // 64)).astype(NPBF)
    ALF = np.zeros((16, 128, 2, 128), np.float32)
    s0 = 32 * (3 - j)
    s_ = np.arange(128)[None, :]
    for i in range(16):
        cur = (96 + 2 * i + (kq >= 64))[:, None]
        al = ((s_ >= s0) & (s_ <= cur)).astype(np.float32)
        bonus = np.zeros((128, 128), np.float32)
        bonus = np.maximum(bonus, np.where(s_ == s0, 16384.0, 0.0))
        bonus = np.maximum(bonus, np.where((s_ == cur - 1) & (cur - 1 >= s0), 65536.0, 0.0))
        bonus = np.maximum(bonus, np.where(s_ == cur, 32768.0, 0.0))
        ALF[i, :, 0, :] = al
        ALF[i, :, 1, :] = (al - 1.0) + bonus
    out["ALF"] = ALF
    _CONST_CACHE[j] = out
    return out


def _win(arr, axis, lo, hi):
    n = arr.shape[axis]
    pad_lo = max(0, -lo)
    sl = [slice(None)] * arr.ndim
    sl[axis] = slice(max(lo, 0), min(hi, n))
    seg = arr[tuple(sl)]
    if pad_lo or hi > n:
        pw = [(0, 0)] * arr.ndim
        pw[axis] = (pad_lo, max(0, hi - n))
        seg = np.pad(seg, pw)
    return seg


def p2_kv_inputs(p1res, b):
    cores = [p1res[b * 4 + j] for j in range(4)]
    KTb = np.concatenate([np.asarray(c["KT"]) for c in cores], axis=2)
    Vb = np.concatenate([np.asarray(c["V"]) for c in cores], axis=0)
    kcb = np.concatenate([np.asarray(c["kcT"]) for c in cores], axis=1)
    vcb = np.concatenate([np.asarray(c["vc"]) for c in cores], axis=0)
    one = np.ones((), NPBF)
    ovl = np.zeros((512, 128), NPBF)
    jr = np.arange(512)
    ovl[jr, jr // 4] = 1
    m3 = jr[(jr % 4 == 3) & (jr // 4 + 1 < 128)]
    ovl[m3, m3 // 4 + 1] = 1
    res = []
    for j in range(4):
        d = {}
        for g in range(3):
            r = A_DIL[g]
            M = 128 + T // r
            lo, hi = T * j - 128 * r, T * (j + 1)
            W = _win(KTb[g * 4:(g + 1) * 4], 2, lo, hi)
            d["KA%d" % g] = np.ascontiguousarray(W.reshape(4, 128, M, r).transpose(0, 1, 3, 2).reshape(4, 128, r * M))
            Wv = _win(Vb[:, g * 512:(g + 1) * 512], 0, lo, hi)
            d["VA%d" % g] = np.ascontiguousarray(Wv.reshape(M, r, 512).transpose(1, 0, 2).reshape(r * M // 128, 128, 512))
        end = T * (j + 1)

        def kaug(kt_rows, lo, hi, pos):
            n = hi - lo
            out = np.zeros((2, 68, n), NPBF)
            w = _win(kt_rows, 1, lo, hi)
            out[:, 0:64, :] = w.reshape(2, 64, n)
            out[:, 64:68, :] = _aug_rows(pos).astype(NPBF)[None]
            return out

        def vaug(v_rows, lo, hi):
            n = hi - lo
            w = _win(v_rows, 0, lo, hi).reshape(n, 2, 64)
            valid = ((np.arange(lo, hi) >= 0) & (np.arange(lo, hi) < SEQ)).astype(NPBF)
            out = np.zeros((n, 2, 65), NPBF)
            out[:, :, 0:64] = w
            out[:, :, 64] = valid[:, None]
            return out.reshape(n // 128, 128, 2, 65)

        d["Ksel"] = kaug(KTb[12], end - SEQ, end, np.arange(SEQ))
        d["Vsel"] = vaug(Vb[:, 1536:1664], end - SEQ, end)
        d["Kwin"] = kaug(KTb[13], T * j - 512, end, 6144 - 512 + np.arange(2560))
        d["Vwin"] = vaug(Vb[:, 1664:1792], T * j - 512, end)
        d["Kc"] = kaug(KTb[14], T * j - 128, end, 6144 - 128 + np.arange(2176))
        d["Vc"] = vaug(Vb[:, 1792:1920], T * j - 128, end)
        blo, bhi = 128 * (j + 1) - 512, 128 * (j + 1)
        d["Kcmp"] = kaug(kcb, blo, bhi, 16 * np.arange(512) + 31)
        vw = _win(vcb, 0, blo, bhi).reshape(512, 2, 64)
        ab = np.arange(blo, bhi)
        valid = ((ab >= 0) & (ab < 511)).astype(NPBF)
        vcm = np.zeros((512, 2, 193), NPBF)
        vcm[:, :, 0:64] = vw * valid[:, None, None]
        vcm[:, :, 64] = valid[:, None]
        vcm[:, :, 65:193] = (ovl * valid[:, None])[:, None, :]
        d["Vcmp"] = vcm.reshape(4, 128, 2, 193)
        res.append(d)
    return res


def run_p2(x_full, wts, p1res, stop_after=None):
    nc = get_nc("p2" + str(stop_after), lambda: build_p2(stop_after))
    in_maps = []
    kv = [p2_kv_inputs(p1res, b) for b in range(NBATCH)]
    for c in range(NCORE):
        b, j = divmod(c, 4)
        m = dict(wts)
        m.update(p2_consts(j))
        m.update(kv[b][j])
        m["xT"] = xT_for_core(x_full[b], j, 0)
        in_maps.append(m)
    res = run_bass_kernel_spmd(nc, in_maps, core_ids=list(range(NCORE)), **RUN_KW)
    if res.exec_time_ns is not None:
        print("[p2] exec_time_ns", res.exec_time_ns, flush=True)
    return res.results


def unshard_xT(results, key="xnT"):
    out = np.zeros((NBATCH, SEQ, D), np.float32)
    for c in range(NCORE):
        b, j = divmod(c, 4)
        out[b, j * T:(j + 1) * T] = np.asarray(results[c][key]).reshape(D, T).T
    return out


def kernel(x, w_in, b_in, w_cmp1, w_cmp2, cmp_pos, sinks, w_branch, w_out, ln_g, ln_b):
    x = np.ascontiguousarray(np.asarray(x, np.float32))
    f = lambda a: np.asarray(a, np.float32)
    for l in range(DEPTH):
        w1 = p1_weights(f(w_in[l]), f(b_in[l]), f(w_cmp1[l]), f(w_cmp2[l]), f(cmp_pos[l]))
        p1res = run_p1(x, w1)
        w2 = p2_weights(f(w_in[l]), f(b_in[l]), f(w_branch[l]), f(w_out[l]), f(ln_g[l]), f(ln_b[l]), f(sinks[l]))
        p2res = run_p2(x, w2, p1res)
        x = unshard_xT(p2res)
    return x
```
